# Optimizing a Trainium2 kernel written in Bass

```python
import math
import jax, jax.numpy as jnp
from jax import lax
import numpy as np

D_MODEL = 1024
BATCH = 8
SEQ = 4096
DEPTH = 4

CHUNK = 64
Q_BLOCK = 128
D_FF = 2816
NORM_EPS = 1e-6

A_HEADS = 4
A_QK_DIM = 64
A_V_DIM = 2 * A_QK_DIM
A_WIDTH = A_HEADS * A_V_DIM
A_COLS = 3 * A_WIDTH

B_HEAD_DIM = 64
B_WIDTH = D_MODEL - A_WIDTH
B_HEADS = B_WIDTH // B_HEAD_DIM
B_DECAY_RANK = 64
B_A_RANK = 64
B_GATE_RANK = 128
B_VRES_RANK = 32
B_LN_EPS = 64e-5
B_COLS = 3 * B_WIDTH + B_DECAY_RANK + B_A_RANK + B_GATE_RANK
B_SPLITS = [B_WIDTH, 2 * B_WIDTH, 3 * B_WIDTH, 3 * B_WIDTH + B_DECAY_RANK,
            3 * B_WIDTH + B_DECAY_RANK + B_A_RANK]
EVEN_IN = A_COLS + B_COLS

C_HEADS = 4
C_QK_DIM = D_MODEL // C_HEADS
C_V_DIM = 2 * C_QK_DIM
C_QK_WIDTH = C_HEADS * C_QK_DIM
C_V_WIDTH = C_HEADS * C_V_DIM
ODD_IN = 2 * C_QK_WIDTH + 2 * C_V_WIDTH

N_EVEN = (DEPTH + 1) // 2
N_ODD = DEPTH // 2

kernel_name = 'hybrid_diffattn_rwkv7_retention_macaron'


def rms_norm(x, g, eps=NORM_EPS):
    xf = x.astype(jnp.float32)
    y = xf * lax.rsqrt(jnp.mean(xf * xf, axis=-1, keepdims=True) + eps)
    return (y * g.astype(jnp.float32)).astype(x.dtype)


def swiglu(x, w_gu, w_d):
    gate, up = jnp.split(x @ w_gu, 2, axis=-1)
    return (jax.nn.silu(gate) * up) @ w_d


def token_shift(z, mu):
    z_prev = jnp.pad(z, ((0, 0), (1, 0), (0, 0)))[:, :-1]
    return z + (z_prev - z) * mu


def diff_attention(q, k, v, lam):
    S_ = q.shape[1]
    q = q * (A_QK_DIM ** -0.5)
    slopes = 2.0 ** (-8.0 / A_HEADS * jnp.arange(1, A_HEADS + 1, dtype=jnp.float32))
    pos = jnp.arange(S_)
    chunk_id = pos // CHUNK
    outs = []
    for s0 in range(0, S_, Q_BLOCK):
        kend = s0 + Q_BLOCK
        scores = jnp.einsum('bqhmd,bkhmd->bhmqk', q[:, s0:kend], k[:, :kend]).astype(jnp.float32)
        dist = jnp.abs(pos[s0:kend][:, None] - pos[:kend][None, :]).astype(jnp.float32)
        bias = -slopes[:, None, None, None] * dist
        allowed = chunk_id[:kend][None, :] <= chunk_id[s0:kend][:, None]
        scores = jnp.where(allowed, scores + bias, -jnp.inf)
        p = jax.nn.softmax(scores, axis=-1)
        p = p[:, :, 0] - lam * p[:, :, 1]
        outs.append(jnp.einsum('bhqk,bkhe->bqhe', p.astype(v.dtype), v[:, :kend]))
    return jnp.concatenate(outs, axis=1)


def rwkv7_recurrence(r, w, k, v, kk, a):
    Bn, _, H, N = r.shape

    def step(state, inp):
        r_t, w_t, k_t, v_t, kk_t, a_t = inp
        sa = jnp.einsum('bhvk,bhk->bhv', state, -kk_t)
        state = (state * w_t[:, :, None, :]
                 + sa[..., None] * (kk_t * a_t)[:, :, None, :]
                 + v_t[..., None] * k_t[:, :, None, :])
        return state, jnp.einsum('bhvk,bhk->bhv', state, r_t)

    xs = tuple(jnp.moveaxis(t, 1, 0) for t in (r, w, k, v, kk, a))
    state0 = jnp.zeros((Bn, H, N, N), jnp.float32)
    _, y = lax.scan(step, state0, xs)
    return jnp.moveaxis(y, 0, 1)


def head_group_norm(y, w, b):
    Bn, S_ = y.shape[:2]
    mean = jnp.mean(y, axis=-1, keepdims=True)
    var = jnp.mean(jnp.square(y - mean), axis=-1, keepdims=True)
    yn = (y - mean) * lax.rsqrt(var + B_LN_EPS)
    return yn.reshape(Bn, S_, -1) * w + b


def rwkv7_time_mix(zb, mu, vec, w_up, a_up, g_up, v_first, vres):
    Bn, S_, _ = zb.shape
    f32 = jnp.float32
    zs = token_shift(zb.astype(f32), mu.astype(f32))
    r, k, v, xw, xa, xg = jnp.split(zs, B_SPLITS, axis=-1)
    w0, a0, k_k, k_a, r_k, ln_w, ln_b = vec.astype(f32)
    w = -jax.nn.softplus(-(w0 + jnp.tanh(xw) @ w_up.astype(f32))) - 0.5
    decay = jnp.exp(-jnp.exp(w))
    if vres is not None:
        v0, v_down, v_up = vres
        v = v + (v_first - v) * jax.nn.sigmoid(v0.astype(f32) + (v @ v_down.astype(f32)) @ v_up.astype(f32))
    a = jax.nn.sigmoid(a0 + xa @ a_up.astype(f32))
    g = jax.nn.sigmoid(xg) @ g_up.astype(f32)
    heads = lambda t: t.reshape(Bn, S_, B_HEADS, B_HEAD_DIM)
    kk = heads(k * k_k)
    kk = kk / jnp.maximum(jnp.sqrt(jnp.sum(kk * kk, axis=-1, keepdims=True)), 1e-12)
    k = k * (1.0 + (a - 1.0) * k_a)
    rh, kh, vh = heads(r), heads(k), heads(v)
    y = rwkv7_recurrence(rh, heads(decay), kh, vh, kk, heads(a))
    y = head_group_norm(y, ln_w, ln_b)
    r_k_h = r_k.reshape(B_HEADS, B_HEAD_DIM)
    bonus = jnp.sum(rh * kh * r_k_h, axis=-1, keepdims=True) * vh
    y = y + bonus.reshape(Bn, S_, B_WIDTH)
    return y * g, v


def diff_rwkv_mixer(h, w_in, w_out, lam_vecs, subln, lam_init, mu, vec, w_up, a_up, g_up, v_first, vres):
    Bn, S_, _ = h.shape
    z = h @ w_in
    za, zb = z[..., :A_COLS], z[..., A_COLS:]
    qa, ka, va = jnp.split(za, 3, axis=-1)
    qa = qa.reshape(Bn, S_, A_HEADS, 2, A_QK_DIM)
    ka = ka.reshape(Bn, S_, A_HEADS, 2, A_QK_DIM)
    va = va.reshape(Bn, S_, A_HEADS, A_V_DIM)
    lv = lam_vecs.astype(jnp.float32)
    lam = jnp.exp(jnp.sum(lv[0] * lv[1])) - jnp.exp(jnp.sum(lv[2] * lv[3])) + lam_init
    oa = diff_attention(qa, ka, va, lam)
    oa = rms_norm(oa, subln) * (1.0 - lam_init)
    ob, v_b = rwkv7_time_mix(zb, mu, vec, w_up, a_up, g_up, v_first, vres)
    mix = jnp.concatenate([oa.reshape(Bn, S_, A_WIDTH), ob.astype(h.dtype)], axis=-1)
    return mix @ w_out, v_b


def retention(q, k, v):
    Bn, S_, H, _ = q.shape
    dk, dv = q.shape[-1], v.shape[-1]
    n_chunks = S_ // CHUNK
    log_g = jnp.log(1.0 - 2.0 ** (-5.0 - jnp.arange(H, dtype=jnp.float32)))
    idx = jnp.arange(CHUNK, dtype=jnp.float32)
    intra = jnp.exp(log_g[:, None, None] * jnp.abs(idx[:, None] - idx[None, :]))
    q_decay = jnp.exp(log_g[:, None] * (idx + 1.0))
    k_decay = jnp.exp(log_g[:, None] * (CHUNK - 1.0 - idx))
    chunk_decay = jnp.exp(log_g * CHUNK)

    def to_chunks(t):
        return t.reshape(Bn, n_chunks, CHUNK, H, t.shape[-1]).transpose(1, 0, 3, 2, 4)

    def step(state, inp):
        qc, kc, vc = inp
        scores = jnp.einsum('bhid,bhjd->bhij', qc, kc) * intra
        y = (jnp.einsum('bhij,bhjv->bhiv', scores, vc)
             + jnp.einsum('bhid,bhdv->bhiv', qc, state) * q_decay[:, :, None])
        state = (state * chunk_decay[:, None, None]
                 + jnp.einsum('bhjd,bhjv->bhdv', kc * k_decay[:, :, None], vc))
        return state, y

    state0 = jnp.zeros((Bn, H, dk, dv), jnp.float32)
    _, y = lax.scan(step, state0, (to_chunks(q), to_chunks(k), to_chunks(v)))
    return y.transpose(1, 0, 3, 2, 4).reshape(Bn, S_, H, dv)


def retention_mixer(h, w_in, w_out):
    Bn, S_, _ = h.shape
    z = (h @ w_in).astype(jnp.float32)
    q, k, v, g = jnp.split(z, [C_QK_WIDTH, 2 * C_QK_WIDTH, 2 * C_QK_WIDTH + C_V_WIDTH], axis=-1)
    q = q.reshape(Bn, S_, C_HEADS, C_QK_DIM)
    k = k.reshape(Bn, S_, C_HEADS, C_QK_DIM) * (C_QK_DIM ** -0.5)
    v = v.reshape(Bn, S_, C_HEADS, C_V_DIM)
    y = retention(q, k, v)
    y = y * lax.rsqrt(jnp.mean(y * y, axis=-1, keepdims=True) + NORM_EPS)
    y = jax.nn.silu(g) * y.reshape(Bn, S_, C_V_WIDTH)
    return y.astype(h.dtype) @ w_out


def setup_inputs(seed: int = 0) -> dict:
    key = jax.random.key(seed)
    ks = jax.random.split(key, 24)
    nrm = lambda k, shape, s: s * jax.random.normal(k, shape, jnp.float32)
    x = nrm(ks[0], (BATCH, SEQ, D_MODEL), 1.0)
    norms = 1.0 + nrm(ks[1], (DEPTH, 6, D_MODEL), 0.05)
    ffn_wgu = nrm(ks[2], (DEPTH, 2, D_MODEL, 2 * D_FF), D_MODEL ** -0.5)
    ffn_wd = nrm(ks[3], (DEPTH, 2, D_FF, D_MODEL), D_FF ** -0.5)
    even_w_in = nrm(ks[4], (N_EVEN, D_MODEL, EVEN_IN), D_MODEL ** -0.5)
    even_w_out = nrm(ks[5], (N_EVEN, D_MODEL, D_MODEL), D_MODEL ** -0.5)
    diff_lam = nrm(ks[6], (N_EVEN, 4, A_QK_DIM), 0.1)
    diff_subln = 1.0 + nrm(ks[7], (N_EVEN, A_V_DIM), 0.05)
    rwkv_mu = jax.random.uniform(ks[8], (N_EVEN, B_COLS), jnp.float32)
    vshape = (N_EVEN, B_WIDTH)
    w0 = jnp.linspace(-6.0, -1.0, B_WIDTH, dtype=jnp.float32)[None, :] + nrm(ks[9], vshape, 0.1)
    a0 = nrm(ks[10], vshape, 0.1)
    k_k = 0.85 + nrm(ks[11], vshape, 0.05)
    k_a = 1.0 + nrm(ks[12], vshape, 0.05)
    r_k = nrm(ks[13], vshape, 0.1)
    ln_w = 1.0 + nrm(ks[14], vshape, 0.05)
    ln_b = nrm(ks[15], vshape, 0.02)
    rwkv_vec = jnp.stack([w0, a0, k_k, k_a, r_k, ln_w, ln_b], axis=1)
    rwkv_w_up = nrm(ks[16], (N_EVEN, B_DECAY_RANK, B_WIDTH), 0.1 * B_DECAY_RANK ** -0.5)
    rwkv_a_up = nrm(ks[17], (N_EVEN, B_A_RANK, B_WIDTH), 0.5 * B_A_RANK ** -0.5)
    rwkv_g_up = nrm(ks[18], (N_EVEN, B_GATE_RANK, B_WIDTH), B_GATE_RANK ** -0.5)
    rwkv_v0 = nrm(ks[19], (N_EVEN - 1, B_WIDTH), 0.1)
    rwkv_v_down = nrm(ks[20], (N_EVEN - 1, B_WIDTH, B_VRES_RANK), B_WIDTH ** -0.5)
    rwkv_v_up = nrm(ks[21], (N_EVEN - 1, B_VRES_RANK, B_WIDTH), 0.5 * B_VRES_RANK ** -0.5)
    odd_w_in = nrm(ks[22], (N_ODD, D_MODEL, ODD_IN), D_MODEL ** -0.5)
    odd_w_out = nrm(ks[23], (N_ODD, C_V_WIDTH, D_MODEL), C_V_WIDTH ** -0.5)
    return {'x': x, 'norms': norms, 'ffn_wgu': ffn_wgu, 'ffn_wd': ffn_wd,
            'even_w_in': even_w_in, 'even_w_out': even_w_out,
            'diff_lam': diff_lam, 'diff_subln': diff_subln,
            'rwkv_mu': rwkv_mu, 'rwkv_vec': rwkv_vec,
            'rwkv_w_up': rwkv_w_up, 'rwkv_a_up': rwkv_a_up, 'rwkv_g_up': rwkv_g_up,
            'rwkv_v0': rwkv_v0, 'rwkv_v_down': rwkv_v_down, 'rwkv_v_up': rwkv_v_up,
            'odd_w_in': odd_w_in, 'odd_w_out': odd_w_out}


def reference(x, norms, ffn_wgu, ffn_wd, even_w_in, even_w_out, diff_lam, diff_subln,
              rwkv_mu, rwkv_vec, rwkv_w_up, rwkv_a_up, rwkv_g_up,
              rwkv_v0, rwkv_v_down, rwkv_v_up, odd_w_in, odd_w_out):
    v_first = None
    for i in range(DEPTH):
        g = norms[i]
        x = x + 0.5 * rms_norm(swiglu(rms_norm(x, g[0]), ffn_wgu[i, 0], ffn_wd[i, 0]), g[1])
        h = rms_norm(x, g[2])
        j = i // 2
        if i % 2 == 0:
            lam_init = 0.8 - 0.6 * math.exp(-0.3 * i)
            vres = None if j == 0 else (rwkv_v0[j - 1], rwkv_v_down[j - 1], rwkv_v_up[j - 1])
            mix, v_b = diff_rwkv_mixer(h, even_w_in[j], even_w_out[j], diff_lam[j], diff_subln[j],
                                       lam_init, rwkv_mu[j], rwkv_vec[j], rwkv_w_up[j],
                                       rwkv_a_up[j], rwkv_g_up[j], v_first, vres)
            if j == 0:
                v_first = v_b
        else:
            mix = retention_mixer(h, odd_w_in[j], odd_w_out[j])
        x = x + rms_norm(mix, g[3])
        x = x + 0.5 * rms_norm(swiglu(rms_norm(x, g[4]), ffn_wgu[i, 1], ffn_wd[i, 1]), g[5])
    return x
```

```python
import math
import os
from contextlib import ExitStack
import numpy as np
import ml_dtypes
import concourse.bass as bass
import concourse.mybir as mybir
from concourse.bass_utils import run_bass_kernel_spmd

F32 = mybir.dt.float32
BF16 = mybir.dt.bfloat16
AF = mybir.ActivationFunctionType
ALU = mybir.AluOpType
AX = mybir.AxisListType

D = 1024
DFF = 2816
NF = DFF // 128
EPS = 1e-6
T = 512
N_ROT = 6
PSUM_KEYS = ('pb', 'psT', 'psg', 'psu', 'pso')


class Sched:
    def __init__(self, nc):
        self.nc = nc
        self.ops = []
        self.lastw = {}
        self.readers = {}
        self.last_on_eng = {}

    def _deps(self, r, w):
        deps = set()
        for k in r:
            if k in self.lastw:
                deps.add(self.lastw[k])
            if isinstance(k, tuple) and k[0] in PSUM_KEYS:
                for x in self.readers.get(k, {}).values():
                    deps.add(x)
        for k in w:
            if k in self.lastw:
                deps.add(self.lastw[k])
            for x in self.readers.get(k, {}).values():
                deps.add(x)
        return deps

    def _add(self, eng, fn, r, w, is_dma):
        idx = len(self.ops)
        deps = self._deps(r, w)
        deps.discard(idx)
        self.ops.append([eng, fn, deps, is_dma])
        for k in r:
            self.readers.setdefault(k, {})[(eng, is_dma and idx)] = idx
        for k in w:
            self.lastw[k] = idx
            self.readers[k] = {}
        self.last_on_eng[(eng, is_dma)] = idx
        return idx

    def op(self, eng, fn, r=(), w=()):
        return self._add(eng, fn, r, w, False)

    def dma(self, q, out, in_, r=(), w=()):
        return self._add(q, lambda e: e.dma_start(out=out, in_=in_), r, w, True)

    def barrier(self):
        self.ops.append(['barrier', None, set(self.last_on_eng.values()), False])

    def emit(self, stack):
        nc = self.nc
        engs = {'pe': nc.tensor, 'act': nc.scalar, 'dve': nc.vector, 'pool': nc.gpsimd, 'sp': nc.sync}
        n = len(self.ops)
        need = [False] * n
        for i, (eng, fn, deps, is_dma) in enumerate(self.ops):
            for d in deps:
                de, _, _, ddma = self.ops[d]
                if eng == 'barrier' or not (de == 'pe' and eng == 'pe' and not ddma and not is_dma):
                    need[d] = True
        esem = {e: stack.enter_context(nc.semaphore('s_' + e)) for e in engs}
        ecnt = {e: 0 for e in engs}
        dsem = {q: [stack.enter_context(nc.semaphore('d_%s%d' % (q, j))) for j in range(N_ROT)]
                for q in ('sp', 'pool', 'act')}
        dcnt = {q: [0] * N_ROT for q in dsem}
        dnum = {q: 0 for q in dsem}
        dlast = {q: [None] * N_ROT for q in dsem}
        sig = [None] * n
        waited = {e: {} for e in engs}
        sem_by_id = {}

        def wait(e, s, v):
            k = id(s)
            sem_by_id[k] = s
            if waited[e].get(k, 0) < v:
                engs[e].wait_ge(s, v)
                waited[e][k] = v

        for i, (eng, fn, deps, is_dma) in enumerate(self.ops):
            if eng == 'barrier':
                for e in engs:
                    for d in deps:
                        if sig[d] is not None:
                            wait(e, *sig[d])
                    for q in dsem:
                        for j in range(N_ROT):
                            if dcnt[q][j] > 0:
                                wait(e, dsem[q][j], dcnt[q][j])
                continue
            for d in sorted(deps):
                if sig[d] is None:
                    continue
                wait(eng, *sig[d])
            if is_dma:
                j = dnum[eng] % N_ROT
                dnum[eng] += 1
                if dcnt[eng][j] > 0:
                    wait(eng, dsem[eng][j], dcnt[eng][j])
                ins = fn(engs[eng])
                dcnt[eng][j] += 16
                ins.then_inc(dsem[eng][j], 16)
                sig[i] = (dsem[eng][j], dcnt[eng][j])
            else:
                ins = fn(engs[eng])
                if need[i]:
                    ecnt[eng] += 1
                    ins.then_inc(esem[eng], 1)
                    sig[i] = (esem[eng], ecnt[eng])
        for q in dsem:
            for j in range(N_ROT):
                if dcnt[q][j] > 0:
                    wait('sp', dsem[q][j], dcnt[q][j])
        self.stats = dict(n_ops=n, ecnt=dict(ecnt), dnum=dict(dnum))


class Builder:
    def __init__(self, seq, depth=4):
        self.S = seq
        self.depth = depth
        self.nc = bass.Bass("TRN2", target_bir_lowering=False)
        self.sc = Sched(self.nc)
        self.uid = 0
        self.in_names = []

    def din(self, name, shape, dt=F32):
        self.in_names.append(name)
        return self.nc.dram_tensor(name, list(shape), dt, kind="ExternalInput").ap()

    def sb(self, st, name, shape, dt):
        self.uid += 1
        return st.enter_context(self.nc.sbuf_tensor("%s_%d" % (name, self.uid), list(shape), dt))

    def ps(self, st, name, shape, dt=F32):
        self.uid += 1
        return st.enter_context(self.nc.psum_tensor("%s_%d" % (name, self.uid), list(shape), dt))

    def load_consts(self, st):
        sc = self.sc
        self.ident_bf = self.sb(st, "identb", [128, 128], BF16)
        self.ident_f = self.sb(st, "identf", [128, 128], F32)
        sc.dma('sp', self.ident_f[:], self.c_ident[:], w=['identf'])
        sc.dma('pool', self.ident_bf[:], self.c_ident[:], w=['identb'])

    def rsqrt(self, ap, key, scale, bias):
        sc = self.sc
        sc.op('act', lambda e: e.activation(out=ap, in_=ap, func=AF.Sqrt, bias=float(bias), scale=float(scale)),
              r=[key], w=[key])
        sc.op('dve', lambda e: e.reciprocal(out=ap, in_=ap), r=[key], w=[key])

    def norm_to_hT(self, st_bufs, xt, xkey, gcol, hb, hT, ss, psT, nblk, tag, pkeys=None):
        sc = self.sc
        junk, gb = st_bufs
        for tb in range(nblk):
            sc.op('act', lambda e, tb=tb: e.activation(out=hb[:, tb, :], in_=xt[:, tb, :], func=AF.Square,
                                                     accum_out=ss[:, tb:tb + 1]),
                  r=[xkey], w=[('hb', tb), ('ss', tb)])
            self.rsqrt(ss[:, tb:tb + 1], ('ss', tb), 1.0 / D, EPS)
            sc.op('dve', lambda e, tb=tb: e.scalar_tensor_tensor(out=hb[:, tb, :], in0=xt[:, tb, :],
                                                               scalar=ss[:, tb:tb + 1], in1=gb[:, gcol, :],
                                                               op0=ALU.mult, op1=ALU.mult),
                  r=[xkey, ('ss', tb), 'gb'], w=[('hb', tb)])
        for c in range(8):
            pk = ('psT', c % 2) if pkeys is None else pkeys[c % 2]
            pt = psT[c % 2]
            for tb in range(nblk):
                sc.op('pe', lambda e, c=c, tb=tb, pt=pt: e.transpose(out=pt[:, tb * 128:(tb + 1) * 128],
                                                                  in_=hb[:, tb, c * 128:(c + 1) * 128],
                                                                  identity=self.ident_bf[:]),
                      r=[('hb', tb), 'identb'], w=[pk])
            if c % 2 == 0:
                sc.op('act', lambda e, c=c, pt=pt: e.copy(out=hT[:, c, :], in_=pt[:, :nblk * 128]),
                      r=[pk], w=[('hT', c)])
            else:
                sc.op('dve', lambda e, c=c, pt=pt: e.tensor_copy(out=hT[:, c, :], in_=pt[:, :nblk * 128]),
                      r=[pk], w=[('hT', c)])

    def post_norm_residual(self, ps_banks, pkeys, xt, xkey, tb, gb, gcol, half_scale, ss2, junk2, tmp):
        sc = self.sc
        for hf in range(2):
            sc.op('act', lambda e, hf=hf: e.activation(out=junk2[:], in_=ps_banks[hf][:], func=AF.Square,
                                                     accum_out=ss2[:, hf:hf + 1]),
                  r=[pkeys[hf]], w=['junk2', ('ss2', hf)])
        sc.op('dve', lambda e: e.tensor_tensor(out=ss2[:, 2:3], in0=ss2[:, 0:1], in1=ss2[:, 1:2], op=ALU.add),
              r=[('ss2', 0), ('ss2', 1)], w=[('ss2', 2)])
        self.rsqrt(ss2[:, 2:3], ('ss2', 2), 1.0 / D, EPS)
        if half_scale != 1.0:
            sc.op('dve', lambda e: e.tensor_single_scalar(out=ss2[:, 2:3], in_=ss2[:, 2:3], scalar=half_scale,
                                                        op=ALU.mult), r=[('ss2', 2)], w=[('ss2', 2)])
        for hf in range(2):
            sl = slice(hf * 512, (hf + 1) * 512)
            sc.op('dve', lambda e, hf=hf, sl=sl: e.scalar_tensor_tensor(out=tmp[:], in0=ps_banks[hf][:],
                                                                     scalar=ss2[:, 2:3], in1=gb[:, gcol, sl],
                                                                     op0=ALU.mult, op1=ALU.mult),
                  r=[pkeys[hf], ('ss2', 2), 'gb'], w=['tmp'])
            sc.op('dve', lambda e, sl=sl: e.tensor_tensor(out=xt[:, tb, sl], in0=xt[:, tb, sl], in1=tmp[:],
                                                        op=ALU.add),
                  r=['tmp', xkey], w=[xkey])

    def ffn_phase(self, layer, j, src, dst):
        sc = self.sc
        nc = self.nc
        S = self.S
        nt = S // T
        nblk = T // 128
        g_in, g_out = (0, 1) if j == 0 else (4, 5)
        wgu = self.w_gu[layer, j].rearrange("(c p) n -> p c n", p=128)
        wd = self.w_d[layer, j].rearrange("(f p) n -> p f n", p=128)
        with ExitStack() as st:
            xt = [self.sb(st, "xt", [128, nblk, D], F32) for _ in range(2)]
            hb = self.sb(st, "hb", [128, nblk, D], BF16)
            hT = self.sb(st, "hT", [128, 8, T], BF16)
            aT = self.sb(st, "aT", [128, NF, T], BF16)
            wdb = self.sb(st, "wdb", [128, NF, D], BF16)
            wgb = [self.sb(st, "wgb", [128, 8, 2, 512], BF16) for _ in range(2)]
            gb = self.sb(st, "gb", [128, 2, D], F32)
            junk = self.sb(st, "junk", [128, D], BF16)
            junk2 = self.sb(st, "junk2", [128, 512], BF16)
            tmp = self.sb(st, "tmp", [128, 512], F32)
            sg = [self.sb(st, "sg", [128, T], F32) for _ in range(2)]
            ss = self.sb(st, "ss", [128, 8], F32)
            ss2 = self.sb(st, "ss2", [128, 4], F32)
            psT = [self.ps(st, "psT", [128, 512], BF16) for _ in range(2)]
            psg = [self.ps(st, "psg", [128, 512], F32) for _ in range(2)]
            psu = [self.ps(st, "psu", [128, 512], F32) for _ in range(2)]
            pso = [self.ps(st, "pso", [128, 512], F32) for _ in range(2)]
            for k, gi in enumerate((g_in, g_out)):
                sc.dma('sp', gb[:, k, :], self.norms[layer, gi:gi + 1, :].partition_broadcast(128), w=['gb'])
            for f0 in range(0, NF, 2):
                sc.dma('pool', wdb[:, f0:f0 + 2, :], wd[:, f0:f0 + 2, :], w=[('wd', f0)])
            groups = [(f, min(4, NF - f)) for f in range(0, NF, 4)]
            xsrc = src.rearrange("(n b p) d -> n p b d", p=128, b=nblk)
            xdst = dst.rearrange("(n b p) d -> n p b d", p=128, b=nblk)
            gi_count = 0
            for ti in range(nt):
                xb = xt[ti % 2]
                xkey = ('xt', ti % 2)
                sc.dma('sp', xb[:], xsrc[ti], r=[('x', ti)], w=[xkey])
                self.norm_to_hT((junk, gb), xb, xkey, 0, hb, hT, ss, psT, nblk, 'f')
                hkeys = [('hT', c) for c in range(8)]
                for (f0, nf) in groups:
                    wb = wgb[gi_count % 2]
                    wk = ('wg', gi_count % 2)
                    gi_count += 1
                    sc.dma('pool', wb[:, :, 0, :nf * 128], wgu[:, :, f0 * 128:(f0 + nf) * 128], w=[wk])
                    sc.dma('pool', wb[:, :, 1, :nf * 128], wgu[:, :, DFF + f0 * 128:DFF + (f0 + nf) * 128], w=[wk])
                    for fl in range(nf):
                        f = f0 + fl
                        pg, pu = psg[f % 2], psu[f % 2]
                        kg, ku = ('psg', f % 2), ('psu', f % 2)
                        for c in range(8):
                            sc.op('pe', lambda e, c=c, fl=fl, pg=pg, wb=wb: e.matmul(
                                pg[:], lhsT=wb[:, c, 0, fl * 128:(fl + 1) * 128], rhs=hT[:, c, :],
                                start=(c == 0), stop=(c == 7)), r=[wk, hkeys[c]], w=[kg])
                        for c in range(8):
                            sc.op('pe', lambda e, c=c, fl=fl, pu=pu, wb=wb: e.matmul(
                                pu[:], lhsT=wb[:, c, 1, fl * 128:(fl + 1) * 128], rhs=hT[:, c, :],
                                start=(c == 0), stop=(c == 7)), r=[wk, hkeys[c]], w=[ku])
                        sgb = sg[f % 2]
                        sc.op('act', lambda e, pg=pg, sgb=sgb: e.activation(out=sgb[:], in_=pg[:], func=AF.Silu),
                              r=[kg], w=[('sg', f % 2)])
                        sc.op('dve', lambda e, f=f, pu=pu, sgb=sgb: e.tensor_tensor(
                            out=aT[:, f, :], in0=pu[:], in1=sgb[:], op=ALU.mult),
                              r=[ku, ('sg', f % 2)], w=[('aT', f)])
                for tb in range(nblk):
                    for hf in range(2):
                        for f in range(NF):
                            sc.op('pe', lambda e, f=f, tb=tb, hf=hf: e.matmul(
                                pso[hf][:], lhsT=aT[:, f, tb * 128:(tb + 1) * 128],
                                rhs=wdb[:, f, hf * 512:(hf + 1) * 512], start=(f == 0), stop=(f == NF - 1)),
                                  r=[('aT', f), ('wd', f - f % 2)], w=[('pso', hf)])
                    self.post_norm_residual(pso, [('pso', 0), ('pso', 1)], xb, xkey, tb, gb, 1, 0.5, ss2, junk2, tmp)
                sc.dma('sp', xdst[ti], xb[:], r=[xkey], w=[('x', ti)])
            sc.barrier()

    def proj_group(self, wsrc, col0, ncols, wbufs, cnt, hT, mode, outs):
        sc = self.sc
        wb = wbufs[cnt[0] % len(wbufs)]
        wk = ('wbuf', cnt[0] % len(wbufs))
        cnt[0] += 1
        sc.dma('pool', wb[:, :, :ncols], wsrc[:, :, col0:col0 + ncols], w=[wk])
        nT = hT.shape[2]
        if mode == 'fm':
            for ci in range(ncols // 128):
                pa, pk, evac = outs(ci)
                for c in range(8):
                    sc.op('pe', lambda e, c=c, ci=ci, pa=pa: e.matmul(
                        pa, lhsT=wb[:, c, ci * 128:(ci + 1) * 128], rhs=hT[:, c, :],
                        start=(c == 0), stop=(c == 7)), r=[wk, ('hT', c)], w=[pk])
                evac(ci)
        else:
            for tb in range(nT // 128):
                pa, pk, evac = outs(tb)
                for c in range(8):
                    sc.op('pe', lambda e, c=c, tb=tb, pa=pa: e.matmul(
                        pa, lhsT=hT[:, c, tb * 128:(tb + 1) * 128], rhs=wb[:, c, :ncols],
                        start=(c == 0), stop=(c == 7)), r=[wk, ('hT', c)], w=[pk])
                evac(tb)

    def out_proj_resid(self, srcT, skeys, nchunk, wres, wkey, pb, pkeys, xb, xkey, tb, gb, gcol, scale,
                       ss2, junk2, tmp):
        sc = self.sc
        for hf in range(2):
            for c in range(nchunk):
                if isinstance(wres, list):
                    rhs_ap, wk_ = wres[hf][:, c, :], wkey[hf]
                else:
                    rhs_ap, wk_ = wres[:, c, hf * 512:(hf + 1) * 512], wkey
                sc.op('pe', lambda e, c=c, hf=hf, rhs_ap=rhs_ap: e.matmul(
                    pb[hf][:], lhsT=srcT[:, c, :], rhs=rhs_ap,
                    start=(c == 0), stop=(c == nchunk - 1)), r=[skeys[c], wk_], w=[pkeys[hf]])
        self.post_norm_residual(pb, pkeys, xb, xkey, tb, gb, gcol, scale, ss2, junk2, tmp)

    def ret_phase(self, layer, src, dst):
        sc = self.sc
        S = self.S
        j = layer // 2
        nt = S // T
        nblk = T // 128
        gam = [1.0 - 2.0 ** (-5.0 - h) for h in range(4)]
        w_in = self.odd_w_in[j].rearrange("(c p) n -> p c n", p=128)
        w_out = self.odd_w_out[j].rearrange("(c p) n -> p c n", p=128)
        with ExitStack() as st:
            xt = [self.sb(st, "xt", [128, nblk, D], F32) for _ in range(1)]
            hb = self.sb(st, "hb", [128, nblk, D], BF16)
            hT = self.sb(st, "hT", [128, 8, T], BF16)
            QT = self.sb(st, "QT", [128, 8, T], BF16)
            QsT = self.sb(st, "QsT", [128, 8, T], BF16)
            KT = self.sb(st, "KT", [128, 8, T], BF16)
            Kt = self.sb(st, "Kt", [128, nblk, 1024], BF16)
            Ks = self.sb(st, "Ks", [128, 1024], BF16)
            Vt = self.sb(st, "Vt", [128, nblk, 2048], BF16)
            Gt = self.sb(st, "Gt", [128, nblk, 2048], BF16)
            wbufs = [self.sb(st, "wbuf", [128, 8, 512], BF16) for _ in range(2)]
            wo = self.sb(st, "wo", [128, 16, D], BF16)
            stf = self.sb(st, "stf", [128, 4, 2, 512], F32)
            stb = self.sb(st, "stb", [128, 4, 2, 512], BF16)
            og = self.sb(st, "og", [128, 2048], BF16)
            ogT = self.sb(st, "ogT", [128, 16, 128], BF16)
            PT = [self.sb(st, "PT", [128, 128], BF16) for _ in range(2)]
            maskT = self.sb(st, "maskT", [128, 4, 128], F32)
            qdrow = self.sb(st, "qdrow", [128, 4, T], F32)
            kdc = self.sb(st, "kdc", [128, 4], F32)
            gb = self.sb(st, "gb", [128, 2, D], F32)
            junk = self.sb(st, "junk", [128, D], BF16)
            junk2 = self.sb(st, "junk2", [128, 512], BF16)
            tmp = self.sb(st, "tmp", [128, 512], F32)
            ss = self.sb(st, "ss", [128, 8], F32)
            ss2 = self.sb(st, "ss2", [128, 4], F32)
            ssy = self.sb(st, "ssy", [128, 4], F32)
            psT = [self.ps(st, "psT", [128, 512], BF16) for _ in range(2)]
            pb = [self.ps(st, "pb", [128, 512], F32) for _ in range(6)]
            pk = [('pb', i) for i in range(6)]
            for k, gi in enumerate((2, 3)):
                sc.dma('sp', gb[:, k, :], self.norms[layer, gi:gi + 1, :].partition_broadcast(128), w=['gb'])
            sk = os.environ.get('SKIP', '')
            if 'm' not in sk:
                sc.dma('sp', maskT[:], self.c_ret_maskT.rearrange("h j i -> j h i"), w=['maskT'])
            if 'q' not in sk:
                sc.dma('sp', qdrow[:], self.c_ret_qdrow.rearrange("h p t -> p h t"), w=['qdrow'])
            if 'k' not in sk:
                sc.dma('sp', kdc[:], self.c_ret_kd[:], w=['kdc'])
            for c0 in range(0, 16, 2):
                sc.dma('pool', wo[:, c0:c0 + 2, :], w_out[:, c0:c0 + 2, :], w=['wo'])
            if 's' not in sk:
                sc.op('dve', lambda e: e.memset(stf[:], 0.0), w=['stf'])
                sc.op('dve', lambda e: e.memset(stb[:], 0.0), w=['stb'])
            xsrc = src.rearrange("(n b p) d -> n p b d", p=128, b=nblk)
            xdst = dst.rearrange("(n b p) d -> n p b d", p=128, b=nblk)
            cnt = [0]
            pcnt = [0]

            def nextpb():
                i = pcnt[0] % 2
                pcnt[0] += 1
                return pb[i], pk[i]

            for ti in range(nt):
                xb = xt[0]
                xkey = ('xt', 0)
                sc.dma('sp', xb[:], xsrc[ti], r=[('x', ti)], w=[xkey])
                self.norm_to_hT((junk, gb), xb, xkey, 0, hb, hT, ss, psT, nblk, 'r')
                for grp in range(0 if 'p' in sk else 2):
                    def outs(ci, grp=grp):
                        pa, pkk = nextpb()
                        cc = grp * 4 + ci
                        def evac(ci_, pa=pa, pkk=pkk, cc=cc):
                            sc.op('act', lambda e: e.copy(out=QT[:, cc, :], in_=pa[:]), r=[pkk], w=[('QT', cc)])
                            if 'x' not in sk:
                                sc.op('dve', lambda e: e.tensor_tensor(out=QsT[:, cc, :], in0=QT[:, cc, :],
                                                                      in1=qdrow[:, cc // 2, :], op=ALU.mult),
                                      r=[('QT', cc), 'qdrow'], w=[('QsT', cc)])
                        return pa[:], pkk, evac
                    self.proj_group(w_in, grp * 512, 512, wbufs, cnt, hT, 'fm', outs)
                for grp in range(0 if ('p' in sk or 'y' in sk) else 2):
                    def outs(ci, grp=grp):
                        pa, pkk = nextpb()
                        cc = grp * 4 + ci
                        def evac(ci_, pa=pa, pkk=pkk, cc=cc):
                            sc.op('act', lambda e: e.activation(out=KT[:, cc, :], in_=pa[:], func=AF.Copy, scale=1.0 / 16.0),
                                  r=[pkk], w=[('KT', cc)])
                        return pa[:], pkk, evac
                    self.proj_group(w_in, 1024 + grp * 512, 512, wbufs, cnt, hT, 'fm', outs)
                for grp in range(0 if 't' in sk else 2):
                    def outs(tb, grp=grp):
                        pa, pkk = nextpb()
                        def evac(tb_, pa=pa, pkk=pkk, grp=grp):
                            sc.op('dve', lambda e: e.tensor_copy(out=Kt[:, tb_, grp * 512:(grp + 1) * 512], in_=pa[:]),
                                  r=[pkk], w=[('Kt', tb_)])
                        return pa[:], pkk, evac
                    self.proj_group(w_in, 1024 + grp * 512, 512, wbufs, cnt, hT, 'tm', outs)
                for grp in range(0 if 't' in sk else 4):
                    def outs(tb, grp=grp):
                        pa, pkk = nextpb()
                        def evac(tb_, pa=pa, pkk=pkk, grp=grp):
                            sc.op('act', lambda e: e.copy(out=Vt[:, tb_, grp * 512:(grp + 1) * 512], in_=pa[:]),
                                  r=[pkk], w=[('Vt', tb_)])
                        return pa[:], pkk, evac
                    self.proj_group(w_in, 2048 + grp * 512, 512, wbufs, cnt, hT, 'tm', outs)
                for grp in range(0 if 't' in sk else 4):
                    def outs(tb, grp=grp):
                        pa, pkk = nextpb()
                        def evac(tb_, pa=pa, pkk=pkk, grp=grp):
                            sc.op('act', lambda e: e.activation(out=Gt[:, tb_, grp * 512:(grp + 1) * 512], in_=pa[:],
                                                               func=AF.Silu), r=[pkk], w=[('Gt', tb_)])
                        return pa[:], pkk, evac
                    self.proj_group(w_in, 4096 + grp * 512, 512, wbufs, cnt, hT, 'tm', outs)
                cut = int(os.environ.get('CUT', '99'))
                def do_head(tb, tsl, h, xb=xb, xkey=xkey):
                    if True:
                        pS, kS = pb[2], pk[2]
                        for dc in range(2):
                            sc.op('pe', lambda e, dc=dc, h=h: e.matmul(
                                pS[:, :128], lhsT=KT[:, 2 * h + dc, tsl], rhs=QT[:, 2 * h + dc, tsl],
                                start=(dc == 0), stop=(dc == 1)),
                                  r=[('KT', 2 * h + dc), ('QT', 2 * h + dc)], w=[kS])
                        ptb = PT[h % 2]
                        ptk = ('PT', h % 2)
                        sc.op('dve', lambda e, h=h, ptb=ptb: e.tensor_tensor(out=ptb[:], in0=pS[:, :128],
                                                                            in1=maskT[:, h, :], op=ALU.mult),
                              r=[kS, 'maskT'], w=[ptk])
                        pY, kY = pb[3], pk[3]
                        sc.op('pe', lambda e, h=h, ptb=ptb: e.matmul(
                            pY[:], lhsT=ptb[:], rhs=Vt[:, tb, h * 512:(h + 1) * 512], start=True, stop=False),
                              r=[ptk, ('Vt', tb)], w=[kY])
                        for dc in range(2):
                            sc.op('pe', lambda e, dc=dc, h=h: e.matmul(
                                pY[:], lhsT=QsT[:, 2 * h + dc, tsl], rhs=stb[:, h, dc, :],
                                start=False, stop=(dc == 1)),
                                  r=[('QsT', 2 * h + dc), ('stb', h)], w=[kY])
                        sc.op('act', lambda e, h=h: e.activation(out=junk2[:], in_=pY[:], func=AF.Square,
                                                                accum_out=ssy[:, h:h + 1]),
                              r=[kY], w=['junk2', ('ssy', h)])
                        self.rsqrt(ssy[:, h:h + 1], ('ssy', h), 1.0 / 512, EPS)
                        sc.op('dve', lambda e, h=h: e.scalar_tensor_tensor(
                            out=og[:, h * 512:(h + 1) * 512], in0=pY[:], scalar=ssy[:, h:h + 1],
                            in1=Gt[:, tb, h * 512:(h + 1) * 512], op0=ALU.mult, op1=ALU.mult),
                              r=[kY, ('ssy', h), ('Gt', tb)], w=[('og', h)])
                        if cut < 3:
                            return
                        sc.op('dve', lambda e, h=h: e.tensor_scalar(
                            out=Ks[:, h * 256:(h + 1) * 256], in0=Kt[:, tb, h * 256:(h + 1) * 256],
                            scalar1=kdc[:, h:h + 1], scalar2=None, op0=ALU.mult),
                              r=[('Kt', tb), 'kdc'], w=[('Ks', h)])
                        for dc in range(2):
                            pSt, kSt = pb[4 + dc], pk[4 + dc]
                            sc.op('pe', lambda e, dc=dc, h=h, pSt=pSt: e.matmul(
                                pSt[:], lhsT=Ks[:, h * 256 + dc * 128:h * 256 + (dc + 1) * 128],
                                rhs=Vt[:, tb, h * 512:(h + 1) * 512], start=True, stop=True),
                                  r=[('Ks', h), ('Vt', tb)], w=[kSt])
                            sc.op('dve', lambda e, dc=dc, h=h, pSt=pSt: e.scalar_tensor_tensor(
                                out=stf[:, h, dc, :], in0=stf[:, h, dc, :], scalar=float(gam[h] ** 128),
                                in1=pSt[:], op0=ALU.mult, op1=ALU.add),
                                  r=[kSt, ('stf', h, dc)], w=[('stf', h, dc)])
                            sc.op('act', lambda e, dc=dc, h=h: e.copy(out=stb[:, h, dc, :], in_=stf[:, h, dc, :]),
                                  r=[('stf', h, dc)], w=[('stb', h)])
                def do_out(tb, xb=xb, xkey=xkey):
                    if cut < 4:
                        return
                    for c in range(16):
                        pt = psT[c % 2]
                        ptk2 = ('psT', c % 2)
                        sc.op('pe', lambda e, c=c, pt=pt: e.transpose(out=pt[:, :128], in_=og[:, c * 128:(c + 1) * 128],
                                                                     identity=self.ident_bf[:]),
                              r=[('og', c // 4), 'identb'], w=[ptk2])
                        eng = 'act' if c % 2 == 0 else 'dve'
                        if eng == 'act':
                            sc.op('act', lambda e, c=c, pt=pt: e.copy(out=ogT[:, c, :], in_=pt[:, :128]),
                                  r=[ptk2], w=[('ogT', c)])
                        else:
                            sc.op('dve', lambda e, c=c, pt=pt: e.tensor_copy(out=ogT[:, c, :], in_=pt[:, :128]),
                                  r=[ptk2], w=[('ogT', c)])
                    self.out_proj_resid(ogT, [('ogT', c) for c in range(16)], 16, wo, 'wo', pb[0:2], pk[0:2],
                                        xb, xkey, tb, gb, 1, 1.0, ss2, junk2, tmp)
                for tb in range(nblk if cut >= 2 else 0):
                    for h in range(4):
                        do_head(tb, slice(tb * 128, (tb + 1) * 128), h)
                    do_out(tb)
                sc.dma('sp', xdst[ti], xb[:], r=[xkey], w=[('x', ti)])
            sc.barrier()

    def load_cols(self, st, dst, dkey, src_rows, R, pbank, pkey, rows_tile):
        sc = self.sc
        sc.dma('sp', rows_tile[:R, :], src_rows, w=['rows_tile'])
        sc.op('pe', lambda e: e.transpose(out=pbank[:, :R], in_=rows_tile[:R, :], identity=self.ident_f[:R, :R]),
              r=['rows_tile', 'identf'], w=[pkey])
        sc.op('dve', lambda e: e.tensor_copy(out=dst, in_=pbank[:, :R]), r=[pkey], w=[dkey])

    def even_phase(self, layer, src, dst):
        sc = self.sc
        S = self.S
        j = layer // 2
        TE = int(os.environ.get('TE', '128'))
        nt = S // TE
        nblk = TE // 128
        NB = S // 128
        lam_init = 0.8 - 0.6 * math.exp(-0.3 * layer)
        w_in = self.even_w_in[j].rearrange("(c p) n -> p c n", p=128)
        w_out = self.even_w_out[j].rearrange("(c p) n -> p c n", p=128)
        do_rwkv = os.environ.get('NO_RWKV', '') == ''
        with ExitStack() as st:
            xt = self.sb(st, "xt", [128, nblk, D], F32)
            hb = self.sb(st, "hb", [128, nblk, D], BF16)
            hT = self.sb(st, "hT", [128, 8, TE], BF16)
            qT = self.sb(st, "qT", [128, 4, TE], BF16)
            kTall = self.sb(st, "kTall", [128, 4, S], BF16)
            Vall = self.sb(st, "Vall", [128, NB, 4, 129], BF16)
            wbufs = [self.sb(st, "wbuf", [128, 8, 512], BF16) for _ in range(2)]
            mixb = self.sb(st, "mixb", [128, nblk, 512 if do_rwkv else D], BF16)
            mixT = self.sb(st, "mixT", [128, 8, 128], BF16)
            PTb = [self.sb(st, "PT", [128, 128], BF16) for _ in range(4)]
            dtmp = [self.sb(st, "dtmp", [128, 128], F32) for _ in range(2)]
            abias = self.sb(st, "abias", [128, 4, 32], F32)
            adiag = self.sb(st, "adiag", [128, 4, 128], F32)
            sublnb = self.sb(st, "sublnb", [128, 128], F32)
            lamt = self.sb(st, "lamt", [128, 4, 64], F32)
            lamc = self.sb(st, "lamc", [128, 8], F32)
            sm = self.sb(st, "sm", [128, 8], F32)
            o1 = self.sb(st, "o1", [128, 128], F32)
            o2 = self.sb(st, "o2", [128, 128], F32)
            gb = self.sb(st, "gb", [128, 2, D], F32)
            junk = None
            junk2 = self.sb(st, "junk2", [128, 512], BF16)
            tmp = self.sb(st, "tmp", [128, 512], F32)
            ss = self.sb(st, "ss", [128, 8], F32)
            ss2 = self.sb(st, "ss2", [128, 4], F32)
            psT = [self.ps(st, "psT", [128, 1024], BF16)]
            pb = [self.ps(st, "pb", [128, 512], F32) for _ in range(7)]
            pk = [('pb', i) for i in range(7)]
            for k, gi in enumerate((2, 3)):
                sc.dma('sp', gb[:, k, :], self.norms[layer, gi:gi + 1, :].partition_broadcast(128), w=['gb'])
            sc.dma('sp', abias[:], self.c_att_bias[:], w=['abias'])
            sc.dma('sp', adiag[:], self.c_att_diag[:], w=['adiag'])
            sc.dma('sp', sublnb[:], self.diff_subln[j:j + 1, :].partition_broadcast(128), w=['sublnb'])
            sc.op('dve', lambda e: e.tensor_single_scalar(out=sublnb[:], in_=sublnb[:], scalar=float(1.0 - lam_init),
                                                        op=ALU.mult), r=['sublnb'], w=['sublnb'])
            sc.dma('sp', lamt[:].rearrange("p a b -> p (a b)"),
                   self.diff_lam[j:j + 1].rearrange("o a b -> o (a b)").partition_broadcast(128), w=['lamt'])
            for q in range(2):
                sc.op('dve', lambda e, q=q: e.tensor_tensor(out=o1[:, :64], in0=lamt[:, 2 * q, :], in1=lamt[:, 2 * q + 1, :],
                                                          op=ALU.mult), r=['lamt'], w=['o1'])
                sc.op('dve', lambda e, q=q: e.reduce_sum(out=lamc[:, q:q + 1], in_=o1[:, :64], axis=AX.X),
                      r=['o1'], w=[('lamc', q)])
                sc.op('act', lambda e, q=q: e.activation(out=lamc[:, q:q + 1], in_=lamc[:, q:q + 1], func=AF.Exp),
                      r=[('lamc', q)], w=[('lamc', q)])
            sc.op('dve', lambda e: e.tensor_tensor(out=lamc[:, 2:3], in0=lamc[:, 0:1], in1=lamc[:, 1:2], op=ALU.subtract),
                  r=[('lamc', 0), ('lamc', 1)], w=[('lamc', 2)])
            sc.op('dve', lambda e: e.tensor_single_scalar(out=lamc[:, 2:3], in_=lamc[:, 2:3], scalar=float(lam_init),
                                                        op=ALU.add), r=[('lamc', 2)], w=[('lamc', 2)])
            sc.op('dve', lambda e: e.memset(Vall[:, :, :, 128:129], 1.0), w=['Vones'])
            sc.op('dve', lambda e: e.memset(mixb[:], 0.0), w=[('mixb', 0), ('mixb', 1)])
            rw = self.rwkv_setup(st, layer, pb, pk, TE) if do_rwkv else None
            if rw is not None:
                rw.psT = psT[0]
            xsrc = src.rearrange("(n b p) d -> n p b d", p=128, b=nblk)
            xdst = dst.rearrange("(n b p) d -> n p b d", p=128, b=nblk)
            dbgv = self.dbg.rearrange("(n b p) d -> n p b d", p=128, b=nblk)
            cnt = [0]
            pcnt = [0]

            def nextpb():
                i = pcnt[0] % 2
                pcnt[0] += 1
                return pb[i], pk[i]

            def attn_block(ti, tb, xkey):
                n = ti * nblk + tb
                qsl = slice(tb * 128, (tb + 1) * 128)
                for h in range(4):
                    acc, kacc = pb[3], pk[3]
                    for m in range(2):
                        psl = slice(64 * m, 64 * m + 64)
                        for kb in range(n + 1):
                            if kb < n:
                                slot = (kb + m) % 3
                                bank = (2, 5, 6)[slot]
                                sS = pb[bank][:, 0:128]
                                kS = ('pb', bank)
                            else:
                                slot = 3
                                sS = pb[4][:, m * 128:(m + 1) * 128]
                                kS = ('pb', 4)
                            ksl = slice(kb * 128, (kb + 1) * 128)
                            sc.op('pe', lambda e, h=h, psl=psl, ksl=ksl, sS=sS: e.matmul(
                                sS, lhsT=kTall[psl, h, ksl], rhs=qT[psl, h, qsl], start=True, stop=True),
                                  r=[('kT', h, kb // nblk), ('qT', h)], w=[kS])
                            ptb = PTb[slot]
                            ptk = ('PT', slot)
                            if kb < n:
                                sc.op('act', lambda e, h=h, kb=kb, sS=sS, ptb=ptb: e.activation(
                                    out=ptb[:], in_=sS, func=AF.Exp, bias=abias[:, h, n - kb:n - kb + 1], scale=0.125),
                                      r=[kS, 'abias'], w=[ptk])
                            else:
                                dt_ = dtmp[m]
                                sc.op('dve', lambda e, h=h, sS=sS, dt_=dt_: e.scalar_tensor_tensor(
                                    out=dt_[:], in0=sS, scalar=0.125, in1=adiag[:, h, :], op0=ALU.mult, op1=ALU.add),
                                      r=[kS, 'adiag'], w=[('dtmp', m)])
                                sc.op('act', lambda e, dt_=dt_, ptb=ptb: e.activation(out=ptb[:], in_=dt_[:], func=AF.Exp),
                                      r=[('dtmp', m)], w=[ptk])
                            sc.op('pe', lambda e, h=h, kb=kb, m=m, ptb=ptb: e.matmul(
                                acc[:, m * 130:m * 130 + 129], lhsT=ptb[:], rhs=Vall[:, kb, h, 0:129],
                                start=(kb == 0), stop=(kb == n)),
                                  r=[ptk, ('V', kb), 'Vones'], w=[kacc])
                    sc.op('dve', lambda e: e.tensor_copy(out=sm[:, 0:1], in_=acc[:, 128:129]), r=[kacc], w=['sm'])
                    sc.op('dve', lambda e: e.tensor_copy(out=sm[:, 1:2], in_=acc[:, 258:259]), r=[kacc], w=['sm'])
                    sc.op('dve', lambda e: e.reciprocal(out=sm[:, 2:4], in_=sm[:, 0:2]), r=['sm'], w=['sm'])
                    sc.op('dve', lambda e: e.tensor_tensor(out=sm[:, 4:5], in0=sm[:, 3:4], in1=lamc[:, 2:3], op=ALU.mult),
                          r=['sm', ('lamc', 2)], w=['sm'])
                    sc.op('dve', lambda e: e.tensor_scalar(out=o1[:], in0=acc[:, 130:258], scalar1=sm[:, 4:5], scalar2=None,
                                                         op0=ALU.mult), r=[kacc, 'sm'], w=['o1'])
                    sc.op('dve', lambda e: e.scalar_tensor_tensor(out=o2[:], in0=acc[:, 0:128], scalar=sm[:, 2:3], in1=o1[:],
                                                                op0=ALU.mult, op1=ALU.subtract),
                          r=[kacc, 'sm', 'o1'], w=['o2'])
                    sc.op('act', lambda e: e.activation(out=junk2[:, :128], in_=o2[:], func=AF.Square,
                                                       accum_out=sm[:, 5:6]), r=['o2'], w=['junk2', 'sm5'])
                    self.rsqrt(sm[:, 5:6], 'sm5', 1.0 / 128, EPS)
                    sc.op('dve', lambda e, h=h: e.scalar_tensor_tensor(
                        out=mixb[:, tb, h * 128:(h + 1) * 128], in0=o2[:], scalar=sm[:, 5:6], in1=sublnb[:],
                        op0=ALU.mult, op1=ALU.mult), r=['o2', 'sm5', 'sublnb'], w=[('mixb', tb)])

            def out_block(ti, tb, xkey):
                nca = 4 if do_rwkv else 8
                for c in range(nca):
                    pt = psT[0]
                    sc.op('pe', lambda e, c=c: e.transpose(out=pt[:, c * 128:(c + 1) * 128],
                                                         in_=mixb[:, tb, c * 128:(c + 1) * 128],
                                                         identity=self.ident_bf[:]),
                          r=[('mixb', tb), 'identb'], w=[('psT', 0)])
                sc.op('act', lambda e: e.copy(out=mixT[:, 0:nca, :].rearrange("p c t -> p (c t)"), in_=psT[0][:, 0:nca * 128]),
                      r=[('psT', 0)], w=[('mixT', c) for c in range(nca)])
                if do_rwkv:
                    sc.op('dve', lambda e: e.tensor_copy(out=mixT[:, 4:8, :], in_=rw.mixTr[:, :, tb * 128:(tb + 1) * 128]),
                          r=['mixTr'], w=[('mixT', c) for c in range(4, 8)])
                    if self.dbg_on:
                        n_ = ti * nblk + tb
                        sc.dma('pool', self.dbg[n_ * 128:(n_ + 1) * 128, 512:1024].rearrange("p (c t) -> p c t", t=128),
                               mixT[:, 4:8, :], r=[('mixT', c) for c in range(4, 8)], w=[('dbg2', n_)])
                wres, wkeys = [], []
                for hf in range(2):
                    wb_ = wbufs[cnt[0] % 2]
                    wk_ = ('wbuf', cnt[0] % 2)
                    cnt[0] += 1
                    sc.dma('pool', wb_[:], w_out[:, :, hf * 512:(hf + 1) * 512], w=[wk_])
                    wres.append(wb_)
                    wkeys.append(wk_)
                self.out_proj_resid(mixT, [('mixT', c) for c in range(8)], 8, wres, wkeys, pb[0:2], pk[0:2],
                                    xt, xkey, tb, gb, 1, 1.0, ss2, junk2, tmp)

            for ti in range(nt):
                xkey = ('xt', 0)
                sc.dma('sp', xt[:], xsrc[ti], r=[('x', ti)], w=[xkey])
                self.norm_to_hT((junk, gb), xt, xkey, 0, hb, hT, ss, [psT[0][:, 0:512], psT[0][:, 512:1024]], nblk, 'e',
                                pkeys=[('psT', 0), ('psT', 0)])
                tsl = slice(ti * TE, (ti + 1) * TE)
                def outs_q(ci):
                    pa, pkk = nextpb()
                    def evac(ci_, pa=pa, pkk=pkk):
                        sc.op('act', lambda e: e.copy(out=qT[:, ci_, :], in_=pa[:, :TE]), r=[pkk], w=[('qT', ci_)])
                    return pa[:, :TE], pkk, evac
                self.proj_group(w_in, 0, 512, wbufs, cnt, hT, 'fm', outs_q)
                def outs_k(ci, ti=ti, tsl=tsl):
                    pa, pkk = nextpb()
                    def evac(ci_, pa=pa, pkk=pkk):
                        sc.op('act', lambda e: e.copy(out=kTall[:, ci_, tsl], in_=pa[:, :TE]), r=[pkk], w=[('kT', ci_, ti)])
                    return pa[:, :TE], pkk, evac
                self.proj_group(w_in, 512, 512, wbufs, cnt, hT, 'fm', outs_k)
                def outs_v(tb, ti=ti):
                    pa, pkk = nextpb()
                    def evac(tb_, pa=pa, pkk=pkk):
                        blk = ti * nblk + tb_
                        sc.op('dve', lambda e: e.tensor_copy(out=Vall[:, blk, :, 0:128],
                                                            in_=pa[:].rearrange("p (h e) -> p h e", e=128)),
                              r=[pkk], w=[('V', blk)])
                    return pa[:], pkk, evac
                self.proj_group(w_in, 1024, 512, wbufs, cnt, hT, 'tm', outs_v)
                if do_rwkv:
                    self.rwkv_tile(rw, ti, w_in, wbufs, cnt, hT, nextpb, mixb)
                for tb in range(nblk):
                    attn_block(ti, tb, xkey)
                if self.dbg_on:
                    sc.dma('pool', dbgv[ti][:, :, 0:512], mixb[:, :, 0:512], r=[('mixb', 0), ('mixb', 1)], w=[('dbg', ti)])
                for tb in range(nblk):
                    out_block(ti, tb, xkey)
                sc.dma('sp', xdst[ti], xt[:], r=[xkey], w=[('x', ti)])
            sc.barrier()

    def rwkv_setup(self, st, layer, pb, pk, TE):
        sc = self.sc
        j = layer // 2
        rw = type('RW', (), {})()
        rw.layer, rw.j, rw.pb, rw.pk, rw.TE = layer, j, pb, pk, TE
        NCH = TE // 64
        rw.NCH = NCH
        A = lambda name, shape, dt=F32: self.sb(st, name, shape, dt)
        rw.cols = A("rwcols", [128, 64])
        rw.rows_tile = A("rowst", [32, 128])
        rw.wa_up = A("wa_up", [128, 512])
        rw.g_up = A("g_up", [128, 512])
        rw.BD = A("BD", [128, 128])
        rw.hs = A("hs", [128, 2])
        rw.RKsel = A("RKsel", [128, 4, 2])
        rw.rmask = A("rmask", [128, TE])
        rw.MK = A("MK", [64, 4, 128])
        rw.ML = A("ML", [64, 8, 64])
        rw.I8 = A("I8", [64, 8, 64])
        rw.lnw_b = A("lnw_b", [64, 512])
        rw.lnb_b = A("lnb_b", [64, 512])
        rw.zb = A("zb", [128, 14, TE + 1])
        rw.zs = A("zs", [128, 14, TE])
        rw.ARt = A("ARt", [128, 4, 2, NCH, 2, 64])
        rw.BKt = A("BKt", [128, 4, 2, NCH, 2, 64])
        rw.nhs = A("nhs", [128, 2])
        rw.Ep = A("Ep", [128, 4, TE])
        rw.rk = A("rk", [128, 4, TE])
        rw.tnh = A("tnh", [128, TE])
        rw.sig = A("sig", [128, TE])
        for nm in ("t1", "ew", "cum", "Em", "Epv", "ta", "kk", "kp", "tt"):
            setattr(rw, nm, A(nm, [128, TE]))
        rw.GMb = A("GMb", [64, 8, 128])
        rw.GMk = A("GMk", [64, 8, 128])
        rw.Lm = A("Lm", [64, 8, 64])
        rw.LpA = A("LpA", [64, 8, 64]); rw.LpTA = A("LpTA", [64, 8, 64])
        rw.LpB = A("LpB", [64, 8, 64]); rw.LpTB = A("LpTB", [64, 8, 64])
        rw.NT = A("NT", [64, 8, 64])
        rw.Vc = A("Vc", [64, 512]); rw.Bc = A("Bc", [64, 512]); rw.Kc = A("Kc", [64, 512])
        rw.X = A("X", [64, 512]); rw.U = A("U", [64, 512])
        rw.yc = rw.U; rw.yt = rw.X
        rw.obc = [A("obc", [64, 512], BF16) for _ in range(2)]
        rw.mixTr = A("mixTr", [128, 4, TE], BF16)
        rw.st8 = A("st8", [64, 32])
        rw.ST = A("ST", [128, 4, 64]); rw.tmpS = A("tmpS", [128, 4, 64])
        vec = self.rwkv_vec[j]
        self.load_cols(st, rw.cols[:, 0:14], 'rwcols', self.rwkv_mu[j].rearrange("(c p) -> c p", p=128), 14,
                       pb[4], pk[4], rw.rows_tile)
        self.load_cols(st, rw.cols[:, 14:42], 'rwcols', vec.rearrange("v (c p) -> (v c) p", p=128), 28,
                       pb[5], pk[5], rw.rows_tile)
        rw.vcol = lambda v, hp: rw.cols[:, 14 + 4 * v + hp:15 + 4 * v + hp]
        sc.op('dve', lambda e: e.tensor_single_scalar(out=rw.cols[:, 42:46], in_=rw.cols[:, 14:18], scalar=-1.0, op=ALU.mult),
              r=['rwcols'], w=['rwcols'])
        sc.op('dve', lambda e: e.tensor_scalar(out=rw.cols[:, 46:50], in0=rw.cols[:, 26:30], scalar1=-1.0, scalar2=1.0,
                                             op0=ALU.mult, op1=ALU.add), r=['rwcols'], w=['rwcols'])
        sc.dma('sp', rw.wa_up[0:64, :], self.rwkv_w_up[j], w=['wa_up'])
        sc.dma('sp', rw.wa_up[64:128, :], self.rwkv_a_up[j], w=['wa_up'])
        sc.dma('sp', rw.g_up[:], self.rwkv_g_up[j], w=['g_up'])
        sc.dma('sp', rw.BD[:], self.c_bd[:], w=['BD'])
        sc.dma('sp', rw.hs[:], self.c_hs[:], w=['hs'])
        sc.dma('sp', rw.rmask[:], self.c_rmask[:, :TE], w=['rmask'])
        sc.dma('sp', rw.MK[:], self.c_mk[:], w=['MK'])
        sc.dma('sp', rw.ML[:], self.c_ml[:], w=['ML'])
        sc.dma('sp', rw.I8[:], self.c_i8[:], w=['I8'])
        sc.dma('sp', rw.lnw_b[:], vec[5:6, :].partition_broadcast(64), w=['lnw_b'])
        sc.dma('sp', rw.lnb_b[:], vec[6:7, :].partition_broadcast(64), w=['lnb_b'])
        sc.op('dve', lambda e: e.tensor_single_scalar(out=rw.nhs[:], in_=rw.hs[:], scalar=-1.0, op=ALU.mult), r=['hs'], w=['hs'])
        for hp in range(4):
            sc.op('dve', lambda e, hp=hp: e.tensor_scalar(out=rw.RKsel[:, hp, :], in0=rw.hs[:], scalar1=rw.vcol(4, hp),
                                                        scalar2=None, op0=ALU.mult), r=['hs', 'rwcols'], w=['RKsel'])
        sc.op('dve', lambda e: e.memset(rw.ST[:], 0.0), w=['ST'])
        sc.op('dve', lambda e: e.memset(rw.zb[:], 0.0), w=[('zb', c) for c in range(14)])
        if j == 1:
            rw.vdn = A("vdn", [128, 4, 32]); rw.vup = A("vup", [32, 512])
            rw.vfT = A("vfT", [128, 4, TE]); rw.vdT = A("vdT", [32, TE])
            self.load_cols(st, rw.cols[:, 50:54], 'rwcols', self.rwkv_v0[0].rearrange("(c p) -> c p", p=128), 4,
                           pb[6], pk[6], rw.rows_tile)
            sc.dma('sp', rw.vdn[:], self.rwkv_v_down[0].rearrange("(c p) r -> p c r", p=128), w=['vdn'])
            sc.dma('sp', rw.vup[:], self.rwkv_v_up[0], w=['vup'])
        return rw

    def rwkv_tile(self, rw, ti, w_in, wbufs, cnt, hT, nextpb, mixb):
        sc = self.sc
        TE, NCH, pb, pk, j = rw.TE, rw.NCH, rw.pb, rw.pk, rw.j
        zb, zs = rw.zb, rw.zs
        EXP, MUL, ADD, SUB = AF.Exp, ALU.mult, ALU.add, ALU.subtract

        def shift(cc):
            sc.op('dve', lambda e: e.tensor_tensor(out=rw.tt[:], in0=zb[:, cc, 0:TE], in1=zb[:, cc, 1:TE + 1], op=SUB),
                  r=[('zb', cc)], w=['tt'])
            sc.op('dve', lambda e: e.scalar_tensor_tensor(out=zs[:, cc, :], in0=rw.tt[:], scalar=rw.cols[:, cc:cc + 1],
                                                        in1=zb[:, cc, 1:TE + 1], op0=MUL, op1=ADD),
                  r=['tt', ('zb', cc), 'rwcols'], w=[('zs', cc)])
            sc.op('dve', lambda e: e.tensor_copy(out=zb[:, cc, 0:1], in_=zb[:, cc, TE:TE + 1]), r=[('zb', cc)], w=[('zb', cc)])

        for grp in range(7):
            def outs(ci, grp=grp):
                pa, pkk = nextpb()
                cc = grp * 2 + ci
                def evac(ci_, pa=pa, pkk=pkk, cc=cc):
                    sc.op('act', lambda e: e.copy(out=zb[:, cc, 1:TE + 1], in_=pa[:, :TE]), r=[pkk], w=[('zb', cc)])
                    shift(cc)
                return pa[:, :TE], pkk, evac
            self.proj_group(w_in, 1536 + grp * 256, 256, wbufs, cnt, hT, 'fm', outs)
        rcut = int(os.environ.get('RW_CUT', '99'))
        if rcut < 1:
            return
        sc.op('act', lambda e: e.activation(out=rw.tnh[0:64, :], in_=zs[0:64, 12, :], func=AF.Tanh), r=[('zs', 12)], w=['tnh'])
        sc.op('act', lambda e: e.activation(out=rw.sig[:], in_=zs[:, 13, :], func=AF.Sigmoid), r=[('zs', 13)], w=['sig'])
        pA, kA, pB, kB, pC, kC = pb[4], pk[4], pb[5], pk[5], pb[6], pk[6]
        v3 = lambda ap: ap.rearrange("p (c t) -> p c t", t=64)

        def prep(hp):
            hs_ = slice(hp * 128, (hp + 1) * 128)
            sc.op('pe', lambda e: e.matmul(pA[:, :TE], lhsT=rw.wa_up[0:64, hs_], rhs=rw.tnh[0:64, :], start=True, stop=True),
                  r=['wa_up', 'tnh'], w=[kA])
            sc.op('act', lambda e: e.activation(out=rw.t1[:], in_=pA[:, :TE], func=EXP, bias=rw.cols[:, 42 + hp:43 + hp], scale=-1.0),
                  r=[kA, 'rwcols'], w=['t1'])
            sc.op('act', lambda e: e.activation(out=rw.t1[:], in_=rw.t1[:], func=AF.Ln, bias=1.0), r=['t1'], w=['t1'])
            sc.op('act', lambda e: e.activation(out=rw.ew[:], in_=rw.t1[:], func=EXP, bias=-0.5, scale=-1.0), r=['t1'], w=['ew'])
            sc.op('pe', lambda e: e.matmul(pB[:, :TE], lhsT=rw.wa_up[64:128, hs_], rhs=zs[64:128, 12, :], start=True, stop=True),
                  r=['wa_up', ('zs', 12)], w=[kB])
            sc.op('act', lambda e: e.activation(out=rw.ta[:], in_=pB[:, :TE], func=AF.Sigmoid, bias=rw.vcol(1, hp)),
                  r=[kB, 'rwcols'], w=['ta'])
            sc.op('dve', lambda e: e.tensor_tensor_scan(out=rw.cum[:], data0=rw.rmask[:], data1=rw.ew[:], initial=0.0,
                                                      op0=MUL, op1=ADD), r=['rmask', 'ew'], w=['cum'])
            sc.op('act', lambda e: e.activation(out=rw.Ep[:, hp, :], in_=rw.cum[:], func=EXP, scale=-1.0), r=['cum'], w=[('Ep', hp)])
            sc.op('act', lambda e: e.activation(out=rw.Em[:], in_=rw.cum[:], func=EXP, scale=1.0), r=['cum'], w=['Em'])
            sc.op('dve', lambda e: e.tensor_tensor(out=rw.tt[:], in0=rw.cum[:], in1=rw.ew[:], op=SUB), r=['cum', 'ew'], w=['tt'])
            sc.op('act', lambda e: e.activation(out=rw.Epv[:], in_=rw.tt[:], func=EXP, scale=-1.0), r=['tt'], w=['Epv'])
            sc.op('dve', lambda e: e.tensor_scalar(out=rw.kk[:], in0=zs[:, 4 + hp, :], scalar1=rw.vcol(2, hp), scalar2=None, op0=MUL),
                  r=[('zs', 4 + hp), 'rwcols'], w=['kk'])
            sc.op('dve', lambda e: e.tensor_tensor(out=rw.tt[:], in0=rw.kk[:], in1=rw.kk[:], op=MUL), r=['kk'], w=['tt'])
            sc.op('pe', lambda e: e.matmul(pA[:, :TE], lhsT=rw.BD[:], rhs=rw.tt[:], start=True, stop=True), r=['BD', 'tt'], w=[kA])
            sc.op('act', lambda e: e.activation(out=rw.tt[:], in_=pA[:, :TE], func=AF.Sqrt), r=[kA], w=['tt'])
            sc.op('dve', lambda e: e.tensor_scalar_max(out=rw.tt[:], in0=rw.tt[:], scalar1=1e-12), r=['tt'], w=['tt'])
            sc.op('dve', lambda e: e.reciprocal(out=rw.tt[:], in_=rw.tt[:]), r=['tt'], w=['tt'])
            sc.op('dve', lambda e: e.tensor_tensor(out=rw.kk[:], in0=rw.kk[:], in1=rw.tt[:], op=MUL), r=['kk', 'tt'], w=['kk'])
            sc.op('dve', lambda e: e.tensor_scalar(out=rw.tt[:], in0=rw.ta[:], scalar1=rw.vcol(3, hp),
                                                 scalar2=rw.cols[:, 46 + hp:47 + hp], op0=MUL, op1=ADD),
                  r=['ta', 'rwcols'], w=['tt'])
            sc.op('dve', lambda e: e.tensor_tensor(out=rw.kp[:], in0=zs[:, 4 + hp, :], in1=rw.tt[:], op=MUL),
                  r=[('zs', 4 + hp), 'tt'], w=['kp'])
            sc.op('dve', lambda e: e.tensor_tensor(out=rw.tt[:], in0=rw.kk[:], in1=rw.ta[:], op=MUL), r=['kk', 'ta'], w=['tt'])
            for par in range(2):
                def masked(par):
                    m_ = rw.hs[:, par:par + 1]
                    nm_ = rw.nhs[:, par:par + 1]
                    sc.op('dve', lambda e: e.scalar_tensor_tensor(out=rw.ARt[:, hp, par, :, 0, :], in0=v3(rw.kk[:]), scalar=nm_,
                                                                in1=v3(rw.Epv[:]), op0=MUL, op1=MUL), r=['kk', 'Epv', 'hs'], w=[('ARt', hp)])
                    sc.op('dve', lambda e: e.scalar_tensor_tensor(out=rw.ARt[:, hp, par, :, 1, :], in0=v3(zs[:, hp, :]), scalar=m_,
                                                                in1=v3(rw.Ep[:, hp, :]), op0=MUL, op1=MUL),
                          r=[('zs', hp), ('Ep', hp), 'hs'], w=[('ARt', hp)])
                    sc.op('dve', lambda e: e.scalar_tensor_tensor(out=rw.BKt[:, hp, par, :, 0, :], in0=v3(rw.tt[:]), scalar=m_,
                                                                in1=v3(rw.Em[:]), op0=MUL, op1=MUL), r=['tt', 'Em', 'hs'], w=[('BKt', hp)])
                    sc.op('dve', lambda e: e.scalar_tensor_tensor(out=rw.BKt[:, hp, par, :, 1, :], in0=v3(rw.kp[:]), scalar=m_,
                                                                in1=v3(rw.Em[:]), op0=MUL, op1=MUL), r=['kp', 'Em', 'hs'], w=[('BKt', hp)])
                masked(par)
            sc.op('dve', lambda e: e.tensor_tensor(out=rw.rk[:, hp, :], in0=zs[:, hp, :], in1=rw.kp[:], op=MUL),
                  r=[('zs', hp), 'kp'], w=[('rk', hp)])

        vfv = self.vfirst.rearrange("(c p) s -> p c s", p=128)[:, :, ti * TE:(ti + 1) * TE]
        vkeys = [('zs', 8 + hp) for hp in range(4)]
        if j == 0:
            if os.environ.get('NOVF', '') == '':
                sc.dma('sp', vfv, zs[:, 8:12, :], r=vkeys, w=[('vf', ti)])
        else:
            sc.dma('sp', rw.vfT[:], vfv, r=[('vf', ti)], w=['vfT'])
            for hp in range(4):
                sc.op('pe', lambda e, hp=hp: e.matmul(pB[:32, 0:TE], lhsT=rw.vdn[:, hp, :], rhs=zs[:, 8 + hp, :],
                                                    start=(hp == 0), stop=(hp == 3)), r=['vdn'] + vkeys, w=[kB])
            sc.op('act', lambda e: e.copy(out=rw.vdT[:], in_=pB[:32, 0:TE]), r=[kB], w=['vdT'])
            for hp in range(4):
                def vres(hp):
                    sc.op('pe', lambda e: e.matmul(pC[:, 0:TE], lhsT=rw.vup[:, hp * 128:(hp + 1) * 128], rhs=rw.vdT[:],
                                                 start=True, stop=True), r=['vdT', 'vup'], w=[kC])
                    sc.op('act', lambda e: e.activation(out=rw.tt[:], in_=pC[:, 0:TE], func=AF.Sigmoid,
                                                       bias=rw.cols[:, 50 + hp:51 + hp]), r=[kC, 'rwcols'], w=['tt'])
                    sc.op('dve', lambda e: e.tensor_tensor(out=rw.vfT[:, hp, :], in0=rw.vfT[:, hp, :], in1=zs[:, 8 + hp, :], op=SUB),
                          r=['vfT', ('zs', 8 + hp)], w=['vfT'])
                    sc.op('dve', lambda e: e.tensor_tensor(out=rw.vfT[:, hp, :], in0=rw.vfT[:, hp, :], in1=rw.tt[:], op=MUL),
                          r=['vfT', 'tt'], w=['vfT'])
                    sc.op('dve', lambda e: e.tensor_tensor(out=zs[:, 8 + hp, :], in0=zs[:, 8 + hp, :], in1=rw.vfT[:, hp, :], op=ADD),
                          r=['vfT', ('zs', 8 + hp)], w=[('zs', 8 + hp)])
                vres(hp)
        for hp in range(4):
            prep(hp)
        if rcut < 2:
            return
        ARk = [('ARt', hp) for hp in range(4)]
        BKk = [('BKt', hp) for hp in range(4)]

        def mm8(bank, bkey, lhs_fn, rhs_fn, rkeys, start=True, stop=True):
            for h in range(8):
                sc.op('pe', lambda e, h=h: e.matmul(bank[:64, h * 64:(h + 1) * 64], lhsT=lhs_fn(h), rhs=rhs_fn(h),
                                                  start=start, stop=stop), r=rkeys, w=[bkey])

        def chunk(ch):
            csl = slice(ch * 64, (ch + 1) * 64)
            gch = ti * NCH + ch
            At = lambda h: rw.ARt[:, h // 2, h % 2, ch, 0, :]
            Rt = lambda h: rw.ARt[:, h // 2, h % 2, ch, 1, :]
            AR = lambda h: rw.ARt[:, h // 2, h % 2, ch, :, :].rearrange("p a t -> p (a t)")
            Bt = lambda h: rw.BKt[:, h // 2, h % 2, ch, 0, :]
            Kt = lambda h: rw.BKt[:, h // 2, h % 2, ch, 1, :]
            STh = lambda h: rw.ST[:, h // 2, :]
            hsl = lambda h: slice(h * 64, (h + 1) * 64)
            for (dst, dkey, bank, bkey, srcs, skeys, eng) in (
                    (rw.Vc, 'Vc', pA, kA, lambda hp: [zs[:, 8 + hp, csl]], [('zs', 8 + hp) for hp in range(4)], 'act'),
                    (rw.Bc, 'Bc', pB, kB, lambda hp: [rw.BKt[:, hp, 0, ch, 0, :], rw.BKt[:, hp, 1, ch, 0, :]], BKk, 'dve'),
                    (rw.Kc, 'Kc', pC, kC, lambda hp: [rw.BKt[:, hp, 0, ch, 1, :], rw.BKt[:, hp, 1, ch, 1, :]], BKk, 'act')):
                for hp in range(4):
                    lst = srcs(hp)
                    for qi, src_ap in enumerate(lst):
                        sc.op('pe', lambda e, hp=hp, bank=bank, src_ap=src_ap, qi=qi, nl=len(lst): e.matmul(
                            bank[:64, hp * 128:(hp + 1) * 128], lhsT=src_ap, rhs=self.ident_f[:],
                            start=(qi == 0), stop=(qi == nl - 1)), r=skeys + ['identf'], w=[bkey])
                if eng == 'act':
                    sc.op('act', lambda e, dst=dst, bank=bank: e.copy(out=dst[:], in_=bank[:64, :]), r=[bkey], w=[dkey])
                else:
                    sc.op('dve', lambda e, dst=dst, bank=bank: e.tensor_copy(out=dst[:], in_=bank[:64, :]), r=[bkey], w=[dkey])
            if rcut < 4:
                return
            for q in range(2):
                for (bank, bkey, lf, GM, gkey) in ((pA, kA, Bt, rw.GMb, 'GMb'), (pB, kB, Kt, rw.GMk, 'GMk')):
                    for hh in range(4):
                        h = 4 * q + hh
                        sc.op('pe', lambda e, h=h, hh=hh, bank=bank, lf=lf: e.matmul(
                            bank[:64, hh * 128:(hh + 1) * 128], lhsT=lf(h), rhs=AR(h), start=True, stop=True),
                              r=ARk + BKk, w=[bkey])
                    sc.op('dve', lambda e, q=q, bank=bank, GM=GM: e.tensor_tensor(
                        out=GM[:, 4 * q:4 * q + 4, :], in0=bank[:64, :].rearrange("p (h t) -> p h t", t=128),
                        in1=rw.MK[:], op=MUL), r=[bkey, 'MK'], w=[gkey])
            mm8(pC, kC, At, Bt, ARk + BKk)
            sc.op('dve', lambda e: e.tensor_tensor(out=rw.Lm[:], in0=pC[:64, :].rearrange("p (h t) -> p h t", t=64),
                                                 in1=rw.ML[:], op=MUL), r=[kC, 'ML'], w=['Lm'])
            if rcut < 5:
                return
            sc.op('dve', lambda e: e.tensor_tensor(out=rw.NT[:], in0=rw.I8[:], in1=rw.GMb[:, :, 0:64], op=ADD),
                  r=['I8', 'GMb'], w=['NT'])
            Lp, LpT, kLp, kLpT = rw.Lm, rw.GMb[:, :, 0:64], 'Lm', 'GMb'
            bufs = [(rw.LpA, rw.LpTA, 'LpA', 'LpTA'), (rw.LpB, rw.LpTB, 'LpB', 'LpTB')]
            for it in range(5):
                nLp, nLpT, knLp, knLpT = bufs[it % 2]
                mm8(pA, kA, (lambda h, LpT=LpT: LpT[:, h, :]), (lambda h, Lp=Lp: Lp[:, h, :]), [kLp, kLpT])
                mm8(pB, kB, (lambda h, Lp=Lp: Lp[:, h, :]), (lambda h, LpT=LpT: LpT[:, h, :]), [kLp, kLpT])
                sc.op('act', lambda e, nLp=nLp: e.copy(out=nLp[:].rearrange("p h t -> p (h t)"), in_=pA[:64, :]), r=[kA], w=[knLp])
                sc.op('dve', lambda e, nLpT=nLpT: e.tensor_copy(out=nLpT[:].rearrange("p h t -> p (h t)"), in_=pB[:64, :]),
                      r=[kB], w=[knLpT])
                mm8(pC, kC, (lambda h, nLp=nLp: nLp[:, h, :]), (lambda h: rw.NT[:, h, :]), [knLp, 'NT'])
                sc.op('dve', lambda e: e.tensor_tensor(out=rw.NT[:].rearrange("p h t -> p (h t)"),
                                                     in0=rw.NT[:].rearrange("p h t -> p (h t)"), in1=pC[:64, :], op=ADD),
                      r=[kC, 'NT'], w=['NT'])
                Lp, LpT, kLp, kLpT = nLp, nLpT, knLp, knLpT
            if rcut < 6:
                return
            def mmseq(bank, bkey, terms, rkeys):
                for h in range(8):
                    for i_, (lf_, rf_) in enumerate(terms):
                        sc.op('pe', lambda e, h=h, lf_=lf_, rf_=rf_, i_=i_: e.matmul(
                            bank[:64, h * 64:(h + 1) * 64], lhsT=lf_(h), rhs=rf_(h),
                            start=(i_ == 0), stop=(i_ == len(terms) - 1)), r=rkeys, w=[bkey])
            mmseq(pA, kA, [(At, STh), ((lambda h: rw.GMk[:, h, 0:64]), (lambda h: rw.Vc[:, hsl(h)]))],
                  ARk + ['ST', 'GMk', 'Vc'])
            sc.op('act', lambda e: e.copy(out=rw.X[:], in_=pA[:64, :]), r=[kA], w=['X'])
            mm8(pB, kB, (lambda h: rw.NT[:, h, :]), (lambda h: rw.X[:, hsl(h)]), ['NT', 'X'])
            sc.op('dve', lambda e: e.tensor_copy(out=rw.U[:], in_=pB[:64, :]), r=[kB], w=['U'])
            mmseq(pC, kC, [(Rt, STh), ((lambda h: rw.GMb[:, h, 64:128]), (lambda h: rw.U[:, hsl(h)])),
                           ((lambda h: rw.GMk[:, h, 64:128]), (lambda h: rw.Vc[:, hsl(h)]))],
                  ARk + ['ST', 'GMb', 'U', 'GMk', 'Vc'])
            if rcut < 7:
                return
            for hp in range(4):
                hs_ = slice(hp * 128, (hp + 1) * 128)
                sc.op('pe', lambda e, hs_=hs_: e.matmul(pA[:, hs_], lhsT=rw.Bc[:, hs_], rhs=rw.U[:, hs_], start=True, stop=False),
                      r=['Bc', 'U'], w=[kA])
                sc.op('pe', lambda e, hs_=hs_: e.matmul(pA[:, hs_], lhsT=rw.Kc[:, hs_], rhs=rw.Vc[:, hs_], start=False, stop=True),
                      r=['Kc', 'Vc'], w=[kA])
            pA3 = pA[:].rearrange("p (h t) -> p h t", t=128)
            sc.op('dve', lambda e: e.tensor_tensor(out=rw.tmpS[0:64, :, :], in0=rw.ST[0:64, :, :], in1=pA3[0:64, :, 0:64], op=ADD),
                  r=[kA, 'ST'], w=['tmpS'])
            sc.op('dve', lambda e: e.tensor_tensor(out=rw.tmpS[64:128, :, :], in0=rw.ST[64:128, :, :], in1=pA3[64:128, :, 64:128], op=ADD),
                  r=[kA, 'ST'], w=['tmpS'])
            for hp in range(4):
                sc.op('dve', lambda e, hp=hp: e.tensor_scalar(out=rw.ST[:, hp, :], in0=rw.tmpS[:, hp, :],
                                                            scalar1=rw.Ep[:, hp, ch * 64 + 63:ch * 64 + 64], scalar2=None, op0=MUL),
                      r=['tmpS', ('Ep', hp)], w=['ST'])
            if rcut < 8:
                return
            Y3 = pC[:64, :].rearrange("p (h t) -> p h t", t=64)
            yc3 = rw.yc[:].rearrange("p (h t) -> p h t", t=64)
            yt3 = rw.yt[:].rearrange("p (h t) -> p h t", t=64)
            s8 = rw.st8
            bc = lambda ap: ap.unsqueeze(2).to_broadcast([64, 8, 64])
            sc.op('dve', lambda e: e.reduce_sum(out=s8[:, 0:8], in_=Y3, axis=AX.X), r=[kC], w=['s8a'])
            sc.op('dve', lambda e: e.tensor_single_scalar(out=s8[:, 0:8], in_=s8[:, 0:8], scalar=1.0 / 64, op=MUL), r=['s8a'], w=['s8a'])
            sc.op('dve', lambda e: e.tensor_tensor(out=yc3, in0=Y3, in1=bc(s8[:, 0:8]), op=SUB), r=[kC, 's8a'], w=['U'])
            sc.op('dve', lambda e: e.tensor_tensor(out=rw.yt[:], in0=rw.yc[:], in1=rw.yc[:], op=MUL), r=['U'], w=['X'])
            sc.op('dve', lambda e: e.reduce_sum(out=s8[:, 8:16], in_=yt3, axis=AX.X), r=['X'], w=['s8b'])
            self.rsqrt(s8[:, 8:16], 's8b', 1.0 / 64, 64e-5)
            sc.op('dve', lambda e: e.tensor_tensor(out=yc3, in0=yc3, in1=bc(s8[:, 8:16]), op=MUL), r=['U', 's8b'], w=['U'])
            sc.op('dve', lambda e: e.tensor_tensor(out=rw.yc[:], in0=rw.yc[:], in1=rw.lnw_b[:], op=MUL), r=['U', 'lnw_b'], w=['U'])
            sc.op('dve', lambda e: e.tensor_tensor(out=rw.yc[:], in0=rw.yc[:], in1=rw.lnb_b[:], op=ADD), r=['U', 'lnb_b'], w=['U'])
            for hp in range(4):
                sc.op('pe', lambda e, hp=hp: e.matmul(pB[:64, 2 * hp:2 * hp + 2], lhsT=rw.rk[:, hp, csl], rhs=rw.RKsel[:, hp, :],
                                                    start=True, stop=True), r=[('rk', hp), 'RKsel'], w=[kB])
            sc.op('act', lambda e: e.copy(out=s8[:, 16:24], in_=pB[:64, 0:8]), r=[kB], w=['s8c'])
            sc.op('dve', lambda e: e.tensor_tensor(out=yt3, in0=rw.Vc[:].rearrange("p (h t) -> p h t", t=64), in1=bc(s8[:, 16:24]), op=MUL),
                  r=['Vc', 's8c'], w=['X'])
            sc.op('dve', lambda e: e.tensor_tensor(out=rw.yc[:], in0=rw.yc[:], in1=rw.yt[:], op=ADD), r=['U', 'X'], w=['U'])
            sc.op('pe', lambda e: e.matmul(pA[:64, :], lhsT=rw.sig[:, csl], rhs=rw.g_up[:], start=True, stop=True),
                  r=['sig', 'g_up'], w=[kA])
            ob = rw.obc[gch % 2]
            okey = ('obc', gch % 2)
            sc.op('dve', lambda e: e.tensor_tensor(out=ob[:], in0=rw.yc[:], in1=pA[:64, :], op=MUL), r=['U', kA], w=[okey])
            for c in range(4):
                sc.op('pe', lambda e, c=c: e.transpose(out=rw.psT[:, c * 64:(c + 1) * 64], in_=ob[:, c * 128:(c + 1) * 128],
                                                     identity=self.ident_bf[0:64, 0:64]), r=[okey, 'identb'], w=[('psT', 0)])
            sc.op('act', lambda e: e.copy(out=rw.mixTr[:, :, ch * 64:(ch + 1) * 64],
                                         in_=rw.psT[:, 0:256].rearrange("p (c t) -> p c t", t=64)),
                  r=[('psT', 0)], w=['mixTr'])

        for ch in range(NCH):
            chunk(ch)

    def build(self, phases):
        nc = self.nc
        S = self.S
        self.x_in = self.din("x", [S, D])
        self.norms = self.din("norms", [4, 6, D])
        self.w_gu = self.din("ffn_wgu", [4, 2, D, 2 * DFF])
        self.w_d = self.din("ffn_wd", [4, 2, DFF, D])
        self.c_ident = self.din("c_ident", [128, 128])
        self.odd_w_in = self.din("odd_w_in", [2, D, 6144])
        self.odd_w_out = self.din("odd_w_out", [2, 2048, D])
        self.c_ret_maskT = self.din("c_ret_maskT", [4, 128, 128])
        self.c_ret_qdrow = self.din("c_ret_qdrow", [4, 128, T])
        self.c_ret_kd = self.din("c_ret_kd", [128, 4])
        self.even_w_in = self.din("even_w_in", [2, D, 3328])
        self.even_w_out = self.din("even_w_out", [2, D, D])
        self.diff_lam = self.din("diff_lam", [2, 4, 64])
        self.diff_subln = self.din("diff_subln", [2, 128])
        self.c_att_bias = self.din("c_att_bias", [128, 4, 32])
        self.rwkv_mu = self.din("rwkv_mu", [2, 1792])
        self.rwkv_vec = self.din("rwkv_vec", [2, 7, 512])
        self.rwkv_w_up = self.din("rwkv_w_up", [2, 64, 512])
        self.rwkv_a_up = self.din("rwkv_a_up", [2, 64, 512])
        self.rwkv_g_up = self.din("rwkv_g_up", [2, 128, 512])
        self.rwkv_v0 = self.din("rwkv_v0", [1, 512])
        self.rwkv_v_down = self.din("rwkv_v_down", [1, 512, 32])
        self.rwkv_v_up = self.din("rwkv_v_up", [1, 32, 512])
        self.c_bd = self.din("c_bd", [128, 128])
        self.c_hs = self.din("c_hs", [128, 2])
        self.c_rmask = self.din("c_rmask", [128, 256])
        self.c_mk = self.din("c_mk", [64, 4, 128])
        self.c_ml = self.din("c_ml", [64, 8, 64])
        self.c_i8 = self.din("c_i8", [64, 8, 64])
        self.vfirst = nc.dram_tensor("vfirst", [512, S], F32, kind="ExternalOutput").ap()
        self.c_att_diag = self.din("c_att_diag", [128, 4, 128])
        self.dbg_on = os.environ.get('DBG', '') != ''
        if self.dbg_on:
            self.dbg = nc.dram_tensor("dbg", [S, D], F32, kind="ExternalOutput").ap()
        else:
            self.dbg = self.x_in
        self.out = nc.dram_tensor("out", [S, D], F32, kind="ExternalOutput").ap()
        with ExitStack() as st:
            self.load_consts(st)
            src = self.x_in
            for ph in phases:
                if ph[0] == 'ffn':
                    self.ffn_phase(ph[1], ph[2], src, self.out)
                elif ph[0] == 'ret':
                    self.ret_phase(ph[1], src, self.out)
                elif ph[0] == 'even':
                    self.even_phase(ph[1], src, self.out)
                src = self.out
            self.sc.emit(st)
        return nc


def host_consts():
    c = {"c_ident": np.eye(128, dtype=np.float32)}
    idx = np.arange(128)
    i = idx[None, :]
    jj = idx[:, None]
    maskT = np.zeros((4, 128, 128), np.float64)
    qd = np.zeros((4, 128), np.float64)
    kd = np.zeros((128, 4), np.float64)
    for h in range(4):
        g = 1.0 - 2.0 ** (-5.0 - h)
        same = (i // 64) == (jj // 64)
        cross = (jj < 64) & (i >= 64)
        maskT[h] = np.where(same, g ** np.abs(i - jj), np.where(cross, g ** (i - jj).clip(0), 0.0))
        qd[h] = g ** (idx + 1.0)
        kd[:, h] = g ** (127.0 - idx) / 16.0
    c["c_ret_maskT"] = maskT.astype(np.float32)
    c["c_ret_qdrow"] = np.ascontiguousarray(
        np.broadcast_to(np.tile(qd, (1, T // 128))[:, None, :], (4, 128, T))).astype(np.float32)
    c["c_ret_kd"] = kd.astype(np.float32)
    ab = np.zeros((128, 4, 32), np.float64)
    ad = np.zeros((128, 4, 128), np.float64)
    kl = idx[:, None]
    ql = idx[None, :]
    for h in range(4):
        sl = 2.0 ** (-8.0 / 4 * (h + 1))
        for dl in range(32):
            ab[:, h, dl] = -sl * (128.0 * dl - idx)
        allowed = (kl // 64) <= (ql // 64)
        ad[:, h, :] = np.where(allowed, -sl * np.abs(ql - kl) + sl * ql, -30000.0)
    c["c_att_bias"] = ab.astype(np.float32)
    p = np.arange(128)
    c["c_bd"] = ((p[:, None] // 64) == (p[None, :] // 64)).astype(np.float32)
    c["c_hs"] = np.stack([(p < 64), (p >= 64)], axis=1).astype(np.float32)
    c["c_rmask"] = np.broadcast_to(((np.arange(256) % 64) != 0).astype(np.float32)[None, :], (128, 256)).copy()
    s64 = np.arange(64)[:, None]
    t64 = np.arange(64)[None, :]
    mk = np.concatenate([(s64 < t64), (s64 <= t64)], axis=1).astype(np.float32)
    c["c_mk"] = np.broadcast_to(mk[:, None, :], (64, 4, 128)).copy()
    c["c_ml"] = np.broadcast_to((s64 > t64).astype(np.float32)[:, None, :], (64, 8, 64)).copy()
    c["c_i8"] = np.broadcast_to(np.eye(64, dtype=np.float32)[:, None, :], (64, 8, 64)).copy()
    c["c_att_diag"] = ad.astype(np.float32)
    return c


_CACHE = {}


def kernel(**inputs):
    x = np.ascontiguousarray(inputs['x'], dtype=np.float32)
    B, S, _ = x.shape
    phases = []
    for i in range(4):
        phases += [('ffn', i, 0), ('even', i) if i % 2 == 0 else ('ret', i), ('ffn', i, 1)]
    b = Builder(S)
    nc = b.build(phases)
    names = set(b.in_names)
    shared = {k: np.ascontiguousarray(v, dtype=np.float32) for k, v in inputs.items() if k in names and k != 'x'}
    shared.update({k: v for k, v in host_consts().items() if k in names})
    in_maps = [dict(shared, x=x[i]) for i in range(B)]
    res = run_bass_kernel_spmd(nc, in_maps, core_ids=list(range(B)))
    return np.stack([np.asarray(r["out"]) for r in res.results], axis=0)
```

```python
import math
import os
from contextlib import ExitStack
import numpy as np
import ml_dtypes
import concourse.bass as bass
import concourse.mybir as mybir
from concourse.bass_utils import run_bass_kernel_spmd

F32 = mybir.dt.float32
BF16 = mybir.dt.bfloat16
AF = mybir.ActivationFunctionType
ALU = mybir.AluOpType
AX = mybir.AxisListType

D = 1024
DFF = 2816
NF = DFF // 128
EPS = 1e-6
T = 512
N_ROT = 6
PSUM_KEYS = ('pb', 'psT', 'psg', 'psu', 'pso')


class Sched:
    def __init__(self, nc):
        self.nc = nc
        self.ops = []
        self.lastw = {}
        self.readers = {}
        self.last_on_eng = {}

    def _deps(self, r, w):
        deps = set()
        for k in r:
            if k in self.lastw:
                deps.add(self.lastw[k])
            if isinstance(k, tuple) and k[0] in PSUM_KEYS:
                for x in self.readers.get(k, {}).values():
                    deps.add(x)
        for k in w:
            if k in self.lastw:
                deps.add(self.lastw[k])
            for x in self.readers.get(k, {}).values():
                deps.add(x)
        return deps

    def _add(self, eng, fn, r, w, is_dma):
        idx = len(self.ops)
        deps = self._deps(r, w)
        deps.discard(idx)
        self.ops.append([eng, fn, deps, is_dma])
        for k in r:
            self.readers.setdefault(k, {})[(eng, is_dma and idx)] = idx
        for k in w:
            self.lastw[k] = idx
            self.readers[k] = {}
        self.last_on_eng[(eng, is_dma)] = idx
        return idx

    def op(self, eng, fn, r=(), w=()):
        return self._add(eng, fn, r, w, False)

    def dma(self, q, out, in_, r=(), w=()):
        return self._add(q, lambda e: e.dma_start(out=out, in_=in_), r, w, True)

    def barrier(self):
        self.ops.append(['barrier', None, set(self.last_on_eng.values()), False])

    def emit(self, stack):
        nc = self.nc
        engs = {'pe': nc.tensor, 'act': nc.scalar, 'dve': nc.vector, 'pool': nc.gpsimd, 'sp': nc.sync}
        n = len(self.ops)
        need = [False] * n
        for i, (eng, fn, deps, is_dma) in enumerate(self.ops):
            for d in deps:
                de, _, _, ddma = self.ops[d]
                if eng == 'barrier' or not (de == 'pe' and eng == 'pe' and not ddma and not is_dma):
                    need[d] = True
        esem = {e: stack.enter_context(nc.semaphore('s_' + e)) for e in engs}
        ecnt = {e: 0 for e in engs}
        dsem = {q: [stack.enter_context(nc.semaphore('d_%s%d' % (q, j))) for j in range(N_ROT)]
                for q in ('sp', 'pool', 'act')}
        dcnt = {q: [0] * N_ROT for q in dsem}
        dnum = {q: 0 for q in dsem}
        dlast = {q: [None] * N_ROT for q in dsem}
        sig = [None] * n
        waited = {e: {} for e in engs}
        sem_by_id = {}

        def wait(e, s, v):
            k = id(s)
            sem_by_id[k] = s
            if waited[e].get(k, 0) < v:
                engs[e].wait_ge(s, v)
                waited[e][k] = v

        for i, (eng, fn, deps, is_dma) in enumerate(self.ops):
            if eng == 'barrier':
                for e in engs:
                    for d in deps:
                        if sig[d] is not None:
                            wait(e, *sig[d])
                    for q in dsem:
                        for j in range(N_ROT):
                            if dcnt[q][j] > 0:
                                wait(e, dsem[q][j], dcnt[q][j])
                continue
            for d in sorted(deps):
                if sig[d] is None:
                    continue
                wait(eng, *sig[d])
            if is_dma:
                j = dnum[eng] % N_ROT
                dnum[eng] += 1
                if dcnt[eng][j] > 0:
                    wait(eng, dsem[eng][j], dcnt[eng][j])
                ins = fn(engs[eng])
                dcnt[eng][j] += 16
                ins.then_inc(dsem[eng][j], 16)
                sig[i] = (dsem[eng][j], dcnt[eng][j])
            else:
                ins = fn(engs[eng])
                if need[i]:
                    ecnt[eng] += 1
                    ins.then_inc(esem[eng], 1)
                    sig[i] = (esem[eng], ecnt[eng])
        for q in dsem:
            for j in range(N_ROT):
                if dcnt[q][j] > 0:
                    wait('sp', dsem[q][j], dcnt[q][j])
        self.stats = dict(n_ops=n, ecnt=dict(ecnt), dnum=dict(dnum))


class Builder:
    def __init__(self, seq, depth=4):
        self.S = seq
        self.depth = depth
        self.nc = bass.Bass("TRN2", target_bir_lowering=False)
        self.sc = Sched(self.nc)
        self.uid = 0
        self.in_names = []

    def din(self, name, shape, dt=F32):
        self.in_names.append(name)
        return self.nc.dram_tensor(name, list(shape), dt, kind="ExternalInput").ap()

    def sb(self, st, name, shape, dt):
        self.uid += 1
        return st.enter_context(self.nc.sbuf_tensor("%s_%d" % (name, self.uid), list(shape), dt))

    def ps(self, st, name, shape, dt=F32):
        self.uid += 1
        return st.enter_context(self.nc.psum_tensor("%s_%d" % (name, self.uid), list(shape), dt))

    def load_consts(self, st):
        sc = self.sc
        self.ident_bf = self.sb(st, "identb", [128, 128], BF16)
        self.ident_f = self.sb(st, "identf", [128, 128], F32)
        sc.dma('sp', self.ident_f[:], self.c_ident[:], w=['identf'])
        sc.dma('pool', self.ident_bf[:], self.c_ident[:], w=['identb'])

    def rsqrt(self, ap, key, scale, bias):
        sc = self.sc
        sc.op('act', lambda e: e.activation(out=ap, in_=ap, func=AF.Sqrt, bias=float(bias), scale=float(scale)),
              r=[key], w=[key])
        sc.op('dve', lambda e: e.reciprocal(out=ap, in_=ap), r=[key], w=[key])

    def norm_to_hT(self, st_bufs, xt, xkey, gcol, hb, hT, ss, psT, nblk, tag, pkeys=None):
        sc = self.sc
        junk, gb = st_bufs
        for tb in range(nblk):
            sc.op('act', lambda e, tb=tb: e.activation(out=hb[:, tb, :], in_=xt[:, tb, :], func=AF.Square,
                                                     accum_out=ss[:, tb:tb + 1]),
                  r=[xkey], w=[('hb', tb), ('ss', tb)])
            self.rsqrt(ss[:, tb:tb + 1], ('ss', tb), 1.0 / D, EPS)
            sc.op('dve', lambda e, tb=tb: e.scalar_tensor_tensor(out=hb[:, tb, :], in0=xt[:, tb, :],
                                                               scalar=ss[:, tb:tb + 1], in1=gb[:, gcol, :],
                                                               op0=ALU.mult, op1=ALU.mult),
                  r=[xkey, ('ss', tb), 'gb'], w=[('hb', tb)])
        for c in range(8):
            pk = ('psT', c % 2) if pkeys is None else pkeys[c % 2]
            pt = psT[c % 2]
            for tb in range(nblk):
                sc.op('pe', lambda e, c=c, tb=tb, pt=pt: e.transpose(out=pt[:, tb * 128:(tb + 1) * 128],
                                                                  in_=hb[:, tb, c * 128:(c + 1) * 128],
                                                                  identity=self.ident_bf[:]),
                      r=[('hb', tb), 'identb'], w=[pk])
            if c % 2 == 0:
                sc.op('act', lambda e, c=c, pt=pt: e.copy(out=hT[:, c, :], in_=pt[:, :nblk * 128]),
                      r=[pk], w=[('hT', c)])
            else:
                sc.op('dve', lambda e, c=c, pt=pt: e.tensor_copy(out=hT[:, c, :], in_=pt[:, :nblk * 128]),
                      r=[pk], w=[('hT', c)])

    def post_norm_residual(self, ps_banks, pkeys, xt, xkey, tb, gb, gcol, half_scale, ss2, junk2, tmp):
        sc = self.sc
        for hf in range(2):
            sc.op('act', lambda e, hf=hf: e.activation(out=junk2[:], in_=ps_banks[hf][:], func=AF.Square,
                                                     accum_out=ss2[:, hf:hf + 1]),
                  r=[pkeys[hf]], w=['junk2', ('ss2', hf)])
        sc.op('dve', lambda e: e.tensor_tensor(out=ss2[:, 2:3], in0=ss2[:, 0:1], in1=ss2[:, 1:2], op=ALU.add),
              r=[('ss2', 0), ('ss2', 1)], w=[('ss2', 2)])
        self.rsqrt(ss2[:, 2:3], ('ss2', 2), 1.0 / D, EPS)
        if half_scale != 1.0:
            sc.op('dve', lambda e: e.tensor_single_scalar(out=ss2[:, 2:3], in_=ss2[:, 2:3], scalar=half_scale,
                                                        op=ALU.mult), r=[('ss2', 2)], w=[('ss2', 2)])
        for hf in range(2):
            sl = slice(hf * 512, (hf + 1) * 512)
            sc.op('dve', lambda e, hf=hf, sl=sl: e.scalar_tensor_tensor(out=tmp[:], in0=ps_banks[hf][:],
                                                                     scalar=ss2[:, 2:3], in1=gb[:, gcol, sl],
                                                                     op0=ALU.mult, op1=ALU.mult),
                  r=[pkeys[hf], ('ss2', 2), 'gb'], w=['tmp'])
            sc.op('dve', lambda e, sl=sl: e.tensor_tensor(out=xt[:, tb, sl], in0=xt[:, tb, sl], in1=tmp[:],
                                                        op=ALU.add),
                  r=['tmp', xkey], w=[xkey])

    def ffn_phase(self, layer, j, src, dst):
        sc = self.sc
        nc = self.nc
        S = self.S
        nt = S // T
        nblk = T // 128
        g_in, g_out = (0, 1) if j == 0 else (4, 5)
        wgu = self.w_gu[layer, j].rearrange("(c p) n -> p c n", p=128)
        wd = self.w_d[layer, j].rearrange("(f p) n -> p f n", p=128)
        with ExitStack() as st:
            xt = [self.sb(st, "xt", [128, nblk, D], F32) for _ in range(2)]
            hb = self.sb(st, "hb", [128, nblk, D], BF16)
            hT = self.sb(st, "hT", [128, 8, T], BF16)
            aT = self.sb(st, "aT", [128, NF, T], BF16)
            wdb = self.sb(st, "wdb", [128, NF, D], BF16)
            wgb = [self.sb(st, "wgb", [128, 8, 2, 512], BF16) for _ in range(2)]
            gb = self.sb(st, "gb", [128, 2, D], F32)
            junk = self.sb(st, "junk", [128, D], BF16)
            junk2 = self.sb(st, "junk2", [128, 512], BF16)
            tmp = self.sb(st, "tmp", [128, 512], F32)
            sg = [self.sb(st, "sg", [128, T], F32) for _ in range(2)]
            ss = self.sb(st, "ss", [128, 8], F32)
            ss2 = self.sb(st, "ss2", [128, 4], F32)
            psT = [self.ps(st, "psT", [128, 512], BF16) for _ in range(2)]
            psg = [self.ps(st, "psg", [128, 512], F32) for _ in range(2)]
            psu = [self.ps(st, "psu", [128, 512], F32) for _ in range(2)]
            pso = [self.ps(st, "pso", [128, 512], F32) for _ in range(2)]
            for k, gi in enumerate((g_in, g_out)):
                sc.dma('sp', gb[:, k, :], self.norms[layer, gi:gi + 1, :].partition_broadcast(128), w=['gb'])
            for f0 in range(0, NF, 2):
                sc.dma('pool', wdb[:, f0:f0 + 2, :], wd[:, f0:f0 + 2, :], w=[('wd', f0)])
            groups = [(f, min(4, NF - f)) for f in range(0, NF, 4)]
            xsrc = src.rearrange("(n b p) d -> n p b d", p=128, b=nblk)
            xdst = dst.rearrange("(n b p) d -> n p b d", p=128, b=nblk)
            gi_count = 0
            for ti in range(nt):
                xb = xt[ti % 2]
                xkey = ('xt', ti % 2)
                sc.dma('sp', xb[:], xsrc[ti], r=[('x', ti)], w=[xkey])
                self.norm_to_hT((junk, gb), xb, xkey, 0, hb, hT, ss, psT, nblk, 'f')
                hkeys = [('hT', c) for c in range(8)]
                for (f0, nf) in groups:
                    wb = wgb[gi_count % 2]
                    wk = ('wg', gi_count % 2)
                    gi_count += 1
                    sc.dma('pool', wb[:, :, 0, :nf * 128], wgu[:, :, f0 * 128:(f0 + nf) * 128], w=[wk])
                    sc.dma('pool', wb[:, :, 1, :nf * 128], wgu[:, :, DFF + f0 * 128:DFF + (f0 + nf) * 128], w=[wk])
                    for fl in range(nf):
                        f = f0 + fl
                        pg, pu = psg[f % 2], psu[f % 2]
                        kg, ku = ('psg', f % 2), ('psu', f % 2)
                        for c in range(8):
                            sc.op('pe', lambda e, c=c, fl=fl, pg=pg, wb=wb: e.matmul(
                                pg[:], lhsT=wb[:, c, 0, fl * 128:(fl + 1) * 128], rhs=hT[:, c, :],
                                start=(c == 0), stop=(c == 7)), r=[wk, hkeys[c]], w=[kg])
                        for c in range(8):
                            sc.op('pe', lambda e, c=c, fl=fl, pu=pu, wb=wb: e.matmul(
                                pu[:], lhsT=wb[:, c, 1, fl * 128:(fl + 1) * 128], rhs=hT[:, c, :],
                                start=(c == 0), stop=(c == 7)), r=[wk, hkeys[c]], w=[ku])
                        sgb = sg[f % 2]
                        sc.op('act', lambda e, pg=pg, sgb=sgb: e.activation(out=sgb[:], in_=pg[:], func=AF.Silu),
                              r=[kg], w=[('sg', f % 2)])
                        sc.op('dve', lambda e, f=f, pu=pu, sgb=sgb: e.tensor_tensor(
                            out=aT[:, f, :], in0=pu[:], in1=sgb[:], op=ALU.mult),
                              r=[ku, ('sg', f % 2)], w=[('aT', f)])
                for tb in range(nblk):
                    for hf in range(2):
                        for f in range(NF):
                            sc.op('pe', lambda e, f=f, tb=tb, hf=hf: e.matmul(
                                pso[hf][:], lhsT=aT[:, f, tb * 128:(tb + 1) * 128],
                                rhs=wdb[:, f, hf * 512:(hf + 1) * 512], start=(f == 0), stop=(f == NF - 1)),
                                  r=[('aT', f), ('wd', f - f % 2)], w=[('pso', hf)])
                    self.post_norm_residual(pso, [('pso', 0), ('pso', 1)], xb, xkey, tb, gb, 1, 0.5, ss2, junk2, tmp)
                sc.dma('sp', xdst[ti], xb[:], r=[xkey], w=[('x', ti)])
            sc.barrier()

    def proj_group(self, wsrc, col0, ncols, wbufs, cnt, hT, mode, outs):
        sc = self.sc
        wb = wbufs[cnt[0] % len(wbufs)]
        wk = ('wbuf', cnt[0] % len(wbufs))
        cnt[0] += 1
        sc.dma('pool', wb[:, :, :ncols], wsrc[:, :, col0:col0 + ncols], w=[wk])
        nT = hT.shape[2]
        if mode == 'fm':
            for ci in range(ncols // 128):
                pa, pk, evac = outs(ci)
                for c in range(8):
                    sc.op('pe', lambda e, c=c, ci=ci, pa=pa: e.matmul(
                        pa, lhsT=wb[:, c, ci * 128:(ci + 1) * 128], rhs=hT[:, c, :],
                        start=(c == 0), stop=(c == 7)), r=[wk, ('hT', c)], w=[pk])
                evac(ci)
        else:
            for tb in range(nT // 128):
                pa, pk, evac = outs(tb)
                for c in range(8):
                    sc.op('pe', lambda e, c=c, tb=tb, pa=pa: e.matmul(
                        pa, lhsT=hT[:, c, tb * 128:(tb + 1) * 128], rhs=wb[:, c, :ncols],
                        start=(c == 0), stop=(c == 7)), r=[wk, ('hT', c)], w=[pk])
                evac(tb)

    def out_proj_resid(self, srcT, skeys, nchunk, wres, wkey, pb, pkeys, xb, xkey, tb, gb, gcol, scale,
                       ss2, junk2, tmp):
        sc = self.sc
        for hf in range(2):
            for c in range(nchunk):
                if isinstance(wres, list):
                    rhs_ap, wk_ = wres[hf][:, c, :], wkey[hf]
                else:
                    rhs_ap, wk_ = wres[:, c, hf * 512:(hf + 1) * 512], wkey
                sc.op('pe', lambda e, c=c, hf=hf, rhs_ap=rhs_ap: e.matmul(
                    pb[hf][:], lhsT=srcT[:, c, :], rhs=rhs_ap,
                    start=(c == 0), stop=(c == nchunk - 1)), r=[skeys[c], wk_], w=[pkeys[hf]])
        self.post_norm_residual(pb, pkeys, xb, xkey, tb, gb, gcol, scale, ss2, junk2, tmp)

    def ret_phase(self, layer, src, dst):
        sc = self.sc
        S = self.S
        j = layer // 2
        nt = S // T
        nblk = T // 128
        gam = [1.0 - 2.0 ** (-5.0 - h) for h in range(4)]
        w_in = self.odd_w_in[j].rearrange("(c p) n -> p c n", p=128)
        w_out = self.odd_w_out[j].rearrange("(c p) n -> p c n", p=128)
        with ExitStack() as st:
            xt = [self.sb(st, "xt", [128, nblk, D], F32) for _ in range(1)]
            hb = self.sb(st, "hb", [128, nblk, D], BF16)
            hT = self.sb(st, "hT", [128, 8, T], BF16)
            QT = self.sb(st, "QT", [128, 8, T], BF16)
            QsT = self.sb(st, "QsT", [128, 8, T], BF16)
            KT = self.sb(st, "KT", [128, 8, T], BF16)
            Kt = self.sb(st, "Kt", [128, nblk, 1024], BF16)
            Ks = self.sb(st, "Ks", [128, 1024], BF16)
            Vt = self.sb(st, "Vt", [128, nblk, 2048], BF16)
            Gt = self.sb(st, "Gt", [128, nblk, 2048], BF16)
            wbufs = [self.sb(st, "wbuf", [128, 8, 512], BF16) for _ in range(2)]
            wo = self.sb(st, "wo", [128, 16, D], BF16)
            stf = self.sb(st, "stf", [128, 4, 2, 512], F32)
            stb = self.sb(st, "stb", [128, 4, 2, 512], BF16)
            og = self.sb(st, "og", [128, 2048], BF16)
            ogT = self.sb(st, "ogT", [128, 16, 128], BF16)
            PT = [self.sb(st, "PT", [128, 128], BF16) for _ in range(2)]
            maskT = self.sb(st, "maskT", [128, 4, 128], F32)
            qdrow = self.sb(st, "qdrow", [128, 4, T], F32)
            kdc = self.sb(st, "kdc", [128, 4], F32)
            gb = self.sb(st, "gb", [128, 2, D], F32)
            junk = self.sb(st, "junk", [128, D], BF16)
            junk2 = self.sb(st, "junk2", [128, 512], BF16)
            tmp = self.sb(st, "tmp", [128, 512], F32)
            ss = self.sb(st, "ss", [128, 8], F32)
            ss2 = self.sb(st, "ss2", [128, 4], F32)
            ssy = self.sb(st, "ssy", [128, 4], F32)
            psT = [self.ps(st, "psT", [128, 512], BF16) for _ in range(2)]
            pb = [self.ps(st, "pb", [128, 512], F32) for _ in range(6)]
            pk = [('pb', i) for i in range(6)]
            for k, gi in enumerate((2, 3)):
                sc.dma('sp', gb[:, k, :], self.norms[layer, gi:gi + 1, :].partition_broadcast(128), w=['gb'])
            sk = os.environ.get('SKIP', '')
            if 'm' not in sk:
                sc.dma('sp', maskT[:], self.c_ret_maskT.rearrange("h j i -> j h i"), w=['maskT'])
            if 'q' not in sk:
                sc.dma('sp', qdrow[:], self.c_ret_qdrow.rearrange("h p t -> p h t"), w=['qdrow'])
            if 'k' not in sk:
                sc.dma('sp', kdc[:], self.c_ret_kd[:], w=['kdc'])
            for c0 in range(0, 16, 2):
                sc.dma('pool', wo[:, c0:c0 + 2, :], w_out[:, c0:c0 + 2, :], w=['wo'])
            if 's' not in sk:
                sc.op('dve', lambda e: e.memset(stf[:], 0.0), w=['stf'])
                sc.op('dve', lambda e: e.memset(stb[:], 0.0), w=['stb'])
            xsrc = src.rearrange("(n b p) d -> n p b d", p=128, b=nblk)
            xdst = dst.rearrange("(n b p) d -> n p b d", p=128, b=nblk)
            cnt = [0]
            pcnt = [0]

            def nextpb():
                i = pcnt[0] % 2
                pcnt[0] += 1
                return pb[i], pk[i]

            for ti in range(nt):
                xb = xt[0]
                xkey = ('xt', 0)
                sc.dma('sp', xb[:], xsrc[ti], r=[('x', ti)], w=[xkey])
                self.norm_to_hT((junk, gb), xb, xkey, 0, hb, hT, ss, psT, nblk, 'r')
                for grp in range(0 if 'p' in sk else 2):
                    def outs(ci, grp=grp):
                        pa, pkk = nextpb()
                        cc = grp * 4 + ci
                        def evac(ci_, pa=pa, pkk=pkk, cc=cc):
                            sc.op('act', lambda e: e.copy(out=QT[:, cc, :], in_=pa[:]), r=[pkk], w=[('QT', cc)])
                            if 'x' not in sk:
                                sc.op('dve', lambda e: e.tensor_tensor(out=QsT[:, cc, :], in0=QT[:, cc, :],
                                                                      in1=qdrow[:, cc // 2, :], op=ALU.mult),
                                      r=[('QT', cc), 'qdrow'], w=[('QsT', cc)])
                        return pa[:], pkk, evac
                    self.proj_group(w_in, grp * 512, 512, wbufs, cnt, hT, 'fm', outs)
                for grp in range(0 if ('p' in sk or 'y' in sk) else 2):
                    def outs(ci, grp=grp):
                        pa, pkk = nextpb()
                        cc = grp * 4 + ci
                        def evac(ci_, pa=pa, pkk=pkk, cc=cc):
                            sc.op('act', lambda e: e.activation(out=KT[:, cc, :], in_=pa[:], func=AF.Copy, scale=1.0 / 16.0),
                                  r=[pkk], w=[('KT', cc)])
                        return pa[:], pkk, evac
                    self.proj_group(w_in, 1024 + grp * 512, 512, wbufs, cnt, hT, 'fm', outs)
                for grp in range(0 if 't' in sk else 2):
                    def outs(tb, grp=grp):
                        pa, pkk = nextpb()
                        def evac(tb_, pa=pa, pkk=pkk, grp=grp):
                            sc.op('dve', lambda e: e.tensor_copy(out=Kt[:, tb_, grp * 512:(grp + 1) * 512], in_=pa[:]),
                                  r=[pkk], w=[('Kt', tb_)])
                        return pa[:], pkk, evac
                    self.proj_group(w_in, 1024 + grp * 512, 512, wbufs, cnt, hT, 'tm', outs)
                for grp in range(0 if 't' in sk else 4):
                    def outs(tb, grp=grp):
                        pa, pkk = nextpb()
                        def evac(tb_, pa=pa, pkk=pkk, grp=grp):
                            sc.op('act', lambda e: e.copy(out=Vt[:, tb_, grp * 512:(grp + 1) * 512], in_=pa[:]),
                                  r=[pkk], w=[('Vt', tb_)])
                        return pa[:], pkk, evac
                    self.proj_group(w_in, 2048 + grp * 512, 512, wbufs, cnt, hT, 'tm', outs)
                for grp in range(0 if 't' in sk else 4):
                    def outs(tb, grp=grp):
                        pa, pkk = nextpb()
                        def evac(tb_, pa=pa, pkk=pkk, grp=grp):
                            sc.op('act', lambda e: e.activation(out=Gt[:, tb_, grp * 512:(grp + 1) * 512], in_=pa[:],
                                                               func=AF.Silu), r=[pkk], w=[('Gt', tb_)])
                        return pa[:], pkk, evac
                    self.proj_group(w_in, 4096 + grp * 512, 512, wbufs, cnt, hT, 'tm', outs)
                cut = int(os.environ.get('CUT', '99'))
                def do_head(tb, tsl, h, xb=xb, xkey=xkey):
                    if True:
                        pS, kS = pb[2], pk[2]
                        for dc in range(2):
                            sc.op('pe', lambda e, dc=dc, h=h: e.matmul(
                                pS[:, :128], lhsT=KT[:, 2 * h + dc, tsl], rhs=QT[:, 2 * h + dc, tsl],
                                start=(dc == 0), stop=(dc == 1)),
                                  r=[('KT', 2 * h + dc), ('QT', 2 * h + dc)], w=[kS])
                        ptb = PT[h % 2]
                        ptk = ('PT', h % 2)
                        sc.op('dve', lambda e, h=h, ptb=ptb: e.tensor_tensor(out=ptb[:], in0=pS[:, :128],
                                                                            in1=maskT[:, h, :], op=ALU.mult),
                              r=[kS, 'maskT'], w=[ptk])
                        pY, kY = pb[3], pk[3]
                        sc.op('pe', lambda e, h=h, ptb=ptb: e.matmul(
                            pY[:], lhsT=ptb[:], rhs=Vt[:, tb, h * 512:(h + 1) * 512], start=True, stop=False),
                              r=[ptk, ('Vt', tb)], w=[kY])
                        for dc in range(2):
                            sc.op('pe', lambda e, dc=dc, h=h: e.matmul(
                                pY[:], lhsT=QsT[:, 2 * h + dc, tsl], rhs=stb[:, h, dc, :],
                                start=False, stop=(dc == 1)),
                                  r=[('QsT', 2 * h + dc), ('stb', h)], w=[kY])
                        sc.op('act', lambda e, h=h: e.activation(out=junk2[:], in_=pY[:], func=AF.Square,
                                                                accum_out=ssy[:, h:h + 1]),
                              r=[kY], w=['junk2', ('ssy', h)])
                        self.rsqrt(ssy[:, h:h + 1], ('ssy', h), 1.0 / 512, EPS)
                        sc.op('dve', lambda e, h=h: e.scalar_tensor_tensor(
                            out=og[:, h * 512:(h + 1) * 512], in0=pY[:], scalar=ssy[:, h:h + 1],
                            in1=Gt[:, tb, h * 512:(h + 1) * 512], op0=ALU.mult, op1=ALU.mult),
                              r=[kY, ('ssy', h), ('Gt', tb)], w=[('og', h)])
                        if cut < 3:
                            return
                        sc.op('dve', lambda e, h=h: e.tensor_scalar(
                            out=Ks[:, h * 256:(h + 1) * 256], in0=Kt[:, tb, h * 256:(h + 1) * 256],
                            scalar1=kdc[:, h:h + 1], scalar2=None, op0=ALU.mult),
                              r=[('Kt', tb), 'kdc'], w=[('Ks', h)])
                        for dc in range(2):
                            pSt, kSt = pb[4 + dc], pk[4 + dc]
                            sc.op('pe', lambda e, dc=dc, h=h, pSt=pSt: e.matmul(
                                pSt[:], lhsT=Ks[:, h * 256 + dc * 128:h * 256 + (dc + 1) * 128],
                                rhs=Vt[:, tb, h * 512:(h + 1) * 512], start=True, stop=True),
                                  r=[('Ks', h), ('Vt', tb)], w=[kSt])
                            sc.op('dve', lambda e, dc=dc, h=h, pSt=pSt: e.scalar_tensor_tensor(
                                out=stf[:, h, dc, :], in0=stf[:, h, dc, :], scalar=float(gam[h] ** 128),
                                in1=pSt[:], op0=ALU.mult, op1=ALU.add),
                                  r=[kSt, ('stf', h, dc)], w=[('stf', h, dc)])
                            sc.op('act', lambda e, dc=dc, h=h: e.copy(out=stb[:, h, dc, :], in_=stf[:, h, dc, :]),
                                  r=[('stf', h, dc)], w=[('stb', h)])
                def do_out(tb, xb=xb, xkey=xkey):
                    if cut < 4:
                        return
                    for c in range(16):
                        pt = psT[c % 2]
                        ptk2 = ('psT', c % 2)
                        sc.op('pe', lambda e, c=c, pt=pt: e.transpose(out=pt[:, :128], in_=og[:, c * 128:(c + 1) * 128],
                                                                     identity=self.ident_bf[:]),
                              r=[('og', c // 4), 'identb'], w=[ptk2])
                        eng = 'act' if c % 2 == 0 else 'dve'
                        if eng == 'act':
                            sc.op('act', lambda e, c=c, pt=pt: e.copy(out=ogT[:, c, :], in_=pt[:, :128]),
                                  r=[ptk2], w=[('ogT', c)])
                        else:
                            sc.op('dve', lambda e, c=c, pt=pt: e.tensor_copy(out=ogT[:, c, :], in_=pt[:, :128]),
                                  r=[ptk2], w=[('ogT', c)])
                    self.out_proj_resid(ogT, [('ogT', c) for c in range(16)], 16, wo, 'wo', pb[0:2], pk[0:2],
                                        xb, xkey, tb, gb, 1, 1.0, ss2, junk2, tmp)
                for tb in range(nblk if cut >= 2 else 0):
                    for h in range(4):
                        do_head(tb, slice(tb * 128, (tb + 1) * 128), h)
                    do_out(tb)
                sc.dma('sp', xdst[ti], xb[:], r=[xkey], w=[('x', ti)])
            sc.barrier()

    def load_cols(self, st, dst, dkey, src_rows, R, pbank, pkey, rows_tile):
        sc = self.sc
        sc.dma('sp', rows_tile[:R, :], src_rows, w=['rows_tile'])
        sc.op('pe', lambda e: e.transpose(out=pbank[:, :R], in_=rows_tile[:R, :], identity=self.ident_f[:R, :R]),
              r=['rows_tile', 'identf'], w=[pkey])
        sc.op('dve', lambda e: e.tensor_copy(out=dst, in_=pbank[:, :R]), r=[pkey], w=[dkey])

    def even_phase(self, layer, src, dst):
        sc = self.sc
        S = self.S
        j = layer // 2
        TE = int(os.environ.get('TE', '128'))
        nt = S // TE
        nblk = TE // 128
        NB = S // 128
        lam_init = 0.8 - 0.6 * math.exp(-0.3 * layer)
        w_in = self.even_w_in[j].rearrange("(c p) n -> p c n", p=128)
        w_out = self.even_w_out[j].rearrange("(c p) n -> p c n", p=128)
        do_rwkv = os.environ.get('NO_RWKV', '') == ''
        with ExitStack() as st:
            xt = self.sb(st, "xt", [128, nblk, D], F32)
            hb = self.sb(st, "hb", [128, nblk, D], BF16)
            hT = self.sb(st, "hT", [128, 8, TE], BF16)
            qT = self.sb(st, "qT", [128, 4, TE], BF16)
            kTall = self.sb(st, "kTall", [128, 4, S], BF16)
            Vall = self.sb(st, "Vall", [128, NB, 4, 129], BF16)
            wbufs = [self.sb(st, "wbuf", [128, 8, 512], BF16) for _ in range(2)]
            mixb = self.sb(st, "mixb", [128, nblk, 512 if do_rwkv else D], BF16)
            mixT = self.sb(st, "mixT", [128, 8, 128], BF16)
            PTb = [self.sb(st, "PT", [128, 128], BF16) for _ in range(4)]
            dtmp = [self.sb(st, "dtmp", [128, 128], F32) for _ in range(2)]
            abias = self.sb(st, "abias", [128, 4, 32], F32)
            adiag = self.sb(st, "adiag", [128, 4, 128], F32)
            sublnb = self.sb(st, "sublnb", [128, 128], F32)
            lamt = self.sb(st, "lamt", [128, 4, 64], F32)
            lamc = self.sb(st, "lamc", [128, 8], F32)
            sm = self.sb(st, "sm", [128, 8], F32)
            o1 = self.sb(st, "o1", [128, 128], F32)
            o2 = self.sb(st, "o2", [128, 128], F32)
            gb = self.sb(st, "gb", [128, 2, D], F32)
            junk = None
            junk2 = self.sb(st, "junk2", [128, 512], BF16)
            tmp = self.sb(st, "tmp", [128, 512], F32)
            ss = self.sb(st, "ss", [128, 8], F32)
            ss2 = self.sb(st, "ss2", [128, 4], F32)
            psT = [self.ps(st, "psT", [128, 1024], BF16)]
            pb = [self.ps(st, "pb", [128, 512], F32) for _ in range(7)]
            pk = [('pb', i) for i in range(7)]
            for k, gi in enumerate((2, 3)):
                sc.dma('sp', gb[:, k, :], self.norms[layer, gi:gi + 1, :].partition_broadcast(128), w=['gb'])
            sc.dma('sp', abias[:], self.c_att_bias[:], w=['abias'])
            sc.dma('sp', adiag[:], self.c_att_diag[:], w=['adiag'])
            sc.dma('sp', sublnb[:], self.diff_subln[j:j + 1, :].partition_broadcast(128), w=['sublnb'])
            sc.op('dve', lambda e: e.tensor_single_scalar(out=sublnb[:], in_=sublnb[:], scalar=float(1.0 - lam_init),
                                                        op=ALU.mult), r=['sublnb'], w=['sublnb'])
            sc.dma('sp', lamt[:].rearrange("p a b -> p (a b)"),
                   self.diff_lam[j:j + 1].rearrange("o a b -> o (a b)").partition_broadcast(128), w=['lamt'])
            for q in range(2):
                sc.op('dve', lambda e, q=q: e.tensor_tensor(out=o1[:, :64], in0=lamt[:, 2 * q, :], in1=lamt[:, 2 * q + 1, :],
                                                          op=ALU.mult), r=['lamt'], w=['o1'])
                sc.op('dve', lambda e, q=q: e.reduce_sum(out=lamc[:, q:q + 1], in_=o1[:, :64], axis=AX.X),
                      r=['o1'], w=[('lamc', q)])
                sc.op('act', lambda e, q=q: e.activation(out=lamc[:, q:q + 1], in_=lamc[:, q:q + 1], func=AF.Exp),
                      r=[('lamc', q)], w=[('lamc', q)])
            sc.op('dve', lambda e: e.tensor_tensor(out=lamc[:, 2:3], in0=lamc[:, 0:1], in1=lamc[:, 1:2], op=ALU.subtract),
                  r=[('lamc', 0), ('lamc', 1)], w=[('lamc', 2)])
            sc.op('dve', lambda e: e.tensor_single_scalar(out=lamc[:, 2:3], in_=lamc[:, 2:3], scalar=float(lam_init),
                                                        op=ALU.add), r=[('lamc', 2)], w=[('lamc', 2)])
            sc.op('dve', lambda e: e.memset(Vall[:, :, :, 128:129], 1.0), w=['Vones'])
            sc.op('dve', lambda e: e.memset(mixb[:], 0.0), w=[('mixb', 0), ('mixb', 1)])
            rw = self.rwkv_setup(st, layer, pb, pk, TE) if do_rwkv else None
            if rw is not None:
                rw.psT = psT[0]
            xsrc = src.rearrange("(n b p) d -> n p b d", p=128, b=nblk)
            xdst = dst.rearrange("(n b p) d -> n p b d", p=128, b=nblk)
            dbgv = self.dbg.rearrange("(n b p) d -> n p b d", p=128, b=nblk)
            cnt = [0]
            pcnt = [0]

            def nextpb():
                i = pcnt[0] % 2
                pcnt[0] += 1
                return pb[i], pk[i]

            def attn_block(ti, tb, xkey):
                n = ti * nblk + tb
                qsl = slice(tb * 128, (tb + 1) * 128)
                for h in range(4):
                    acc, kacc = pb[3], pk[3]
                    for m in range(2):
                        psl = slice(64 * m, 64 * m + 64)
                        for kb in range(n + 1):
                            if kb < n:
                                slot = (kb + m) % 3
                                bank = (2, 5, 6)[slot]
                                sS = pb[bank][:, 0:128]
                                kS = ('pb', bank)
                            else:
                                slot = 3
                                sS = pb[4][:, m * 128:(m + 1) * 128]
                                kS = ('pb', 4)
                            ksl = slice(kb * 128, (kb + 1) * 128)
                            sc.op('pe', lambda e, h=h, psl=psl, ksl=ksl, sS=sS: e.matmul(
                                sS, lhsT=kTall[psl, h, ksl], rhs=qT[psl, h, qsl], start=True, stop=True),
                                  r=[('kT', h, kb // nblk), ('qT', h)], w=[kS])
                            ptb = PTb[slot]
                            ptk = ('PT', slot)
                            if kb < n:
                                sc.op('act', lambda e, h=h, kb=kb, sS=sS, ptb=ptb: e.activation(
                                    out=ptb[:], in_=sS, func=AF.Exp, bias=abias[:, h, n - kb:n - kb + 1], scale=0.125),
                                      r=[kS, 'abias'], w=[ptk])
                            else:
                                dt_ = dtmp[m]
                                sc.op('dve', lambda e, h=h, sS=sS, dt_=dt_: e.scalar_tensor_tensor(
                                    out=dt_[:], in0=sS, scalar=0.125, in1=adiag[:, h, :], op0=ALU.mult, op1=ALU.add),
                                      r=[kS, 'adiag'], w=[('dtmp', m)])
                                sc.op('act', lambda e, dt_=dt_, ptb=ptb: e.activation(out=ptb[:], in_=dt_[:], func=AF.Exp),
                                      r=[('dtmp', m)], w=[ptk])
                            sc.op('pe', lambda e, h=h, kb=kb, m=m, ptb=ptb: e.matmul(
                                acc[:, m * 130:m * 130 + 129], lhsT=ptb[:], rhs=Vall[:, kb, h, 0:129],
                                start=(kb == 0), stop=(kb == n)),
                                  r=[ptk, ('V', kb), 'Vones'], w=[kacc])
                    sc.op('dve', lambda e: e.tensor_copy(out=sm[:, 0:1], in_=acc[:, 128:129]), r=[kacc], w=['sm'])
                    sc.op('dve', lambda e: e.tensor_copy(out=sm[:, 1:2], in_=acc[:, 258:259]), r=[kacc], w=['sm'])
                    sc.op('dve', lambda e: e.reciprocal(out=sm[:, 2:4], in_=sm[:, 0:2]), r=['sm'], w=['sm'])
                    sc.op('dve', lambda e: e.tensor_tensor(out=sm[:, 4:5], in0=sm[:, 3:4], in1=lamc[:, 2:3], op=ALU.mult),
                          r=['sm', ('lamc', 2)], w=['sm'])
                    sc.op('dve', lambda e: e.tensor_scalar(out=o1[:], in0=acc[:, 130:258], scalar1=sm[:, 4:5], scalar2=None,
                                                         op0=ALU.mult), r=[kacc, 'sm'], w=['o1'])
                    sc.op('dve', lambda e: e.scalar_tensor_tensor(out=o2[:], in0=acc[:, 0:128], scalar=sm[:, 2:3], in1=o1[:],
                                                                op0=ALU.mult, op1=ALU.subtract),
                          r=[kacc, 'sm', 'o1'], w=['o2'])
                    sc.op('act', lambda e: e.activation(out=junk2[:, :128], in_=o2[:], func=AF.Square,
                                                       accum_out=sm[:, 5:6]), r=['o2'], w=['junk2', 'sm5'])
                    self.rsqrt(sm[:, 5:6], 'sm5', 1.0 / 128, EPS)
                    sc.op('dve', lambda e, h=h: e.scalar_tensor_tensor(
                        out=mixb[:, tb, h * 128:(h + 1) * 128], in0=o2[:], scalar=sm[:, 5:6], in1=sublnb[:],
                        op0=ALU.mult, op1=ALU.mult), r=['o2', 'sm5', 'sublnb'], w=[('mixb', tb)])

            def out_block(ti, tb, xkey):
                nca = 4 if do_rwkv else 8
                for c in range(nca):
                    pt = psT[0]
                    sc.op('pe', lambda e, c=c: e.transpose(out=pt[:, c * 128:(c + 1) * 128],
                                                         in_=mixb[:, tb, c * 128:(c + 1) * 128],
                                                         identity=self.ident_bf[:]),
                          r=[('mixb', tb), 'identb'], w=[('psT', 0)])
                sc.op('act', lambda e: e.copy(out=mixT[:, 0:nca, :].rearrange("p c t -> p (c t)"), in_=psT[0][:, 0:nca * 128]),
                      r=[('psT', 0)], w=[('mixT', c) for c in range(nca)])
                if do_rwkv:
                    sc.op('dve', lambda e: e.tensor_copy(out=mixT[:, 4:8, :], in_=rw.mixTr[:, :, tb * 128:(tb + 1) * 128]),
                          r=['mixTr'], w=[('mixT', c) for c in range(4, 8)])
                    if self.dbg_on:
                        n_ = ti * nblk + tb
                        sc.dma('pool', self.dbg[n_ * 128:(n_ + 1) * 128, 512:1024].rearrange("p (c t) -> p c t", t=128),
                               mixT[:, 4:8, :], r=[('mixT', c) for c in range(4, 8)], w=[('dbg2', n_)])
                wres, wkeys = [], []
                for hf in range(2):
                    wb_ = wbufs[cnt[0] % 2]
                    wk_ = ('wbuf', cnt[0] % 2)
                    cnt[0] += 1
                    sc.dma('pool', wb_[:], w_out[:, :, hf * 512:(hf + 1) * 512], w=[wk_])
                    wres.append(wb_)
                    wkeys.append(wk_)
                self.out_proj_resid(mixT, [('mixT', c) for c in range(8)], 8, wres, wkeys, pb[0:2], pk[0:2],
                                    xt, xkey, tb, gb, 1, 1.0, ss2, junk2, tmp)

            for ti in range(nt):
                xkey = ('xt', 0)
                sc.dma('sp', xt[:], xsrc[ti], r=[('x', ti)], w=[xkey])
                self.norm_to_hT((junk, gb), xt, xkey, 0, hb, hT, ss, [psT[0][:, 0:512], psT[0][:, 512:1024]], nblk, 'e',
                                pkeys=[('psT', 0), ('psT', 0)])
                tsl = slice(ti * TE, (ti + 1) * TE)
                def outs_q(ci):
                    pa, pkk = nextpb()
                    def evac(ci_, pa=pa, pkk=pkk):
                        sc.op('act', lambda e: e.copy(out=qT[:, ci_, :], in_=pa[:, :TE]), r=[pkk], w=[('qT', ci_)])
                    return pa[:, :TE], pkk, evac
                self.proj_group(w_in, 0, 512, wbufs, cnt, hT, 'fm', outs_q)
                def outs_k(ci, ti=ti, tsl=tsl):
                    pa, pkk = nextpb()
                    def evac(ci_, pa=pa, pkk=pkk):
                        sc.op('act', lambda e: e.copy(out=kTall[:, ci_, tsl], in_=pa[:, :TE]), r=[pkk], w=[('kT', ci_, ti)])
                    return pa[:, :TE], pkk, evac
                self.proj_group(w_in, 512, 512, wbufs, cnt, hT, 'fm', outs_k)
                def outs_v(tb, ti=ti):
                    pa, pkk = nextpb()
                    def evac(tb_, pa=pa, pkk=pkk):
                        blk = ti * nblk + tb_
                        sc.op('dve', lambda e: e.tensor_copy(out=Vall[:, blk, :, 0:128],
                                                            in_=pa[:].rearrange("p (h e) -> p h e", e=128)),
                              r=[pkk], w=[('V', blk)])
                    return pa[:], pkk, evac
                self.proj_group(w_in, 1024, 512, wbufs, cnt, hT, 'tm', outs_v)
                if do_rwkv:
                    self.rwkv_tile(rw, ti, w_in, wbufs, cnt, hT, nextpb, mixb)
                for tb in range(nblk):
                    attn_block(ti, tb, xkey)
                if self.dbg_on:
                    sc.dma('pool', dbgv[ti][:, :, 0:512], mixb[:, :, 0:512], r=[('mixb', 0), ('mixb', 1)], w=[('dbg', ti)])
                for tb in range(nblk):
                    out_block(ti, tb, xkey)
                sc.dma('sp', xdst[ti], xt[:], r=[xkey], w=[('x', ti)])
            sc.barrier()

    def rwkv_setup(self, st, layer, pb, pk, TE):
        sc = self.sc
        j = layer // 2
        rw = type('RW', (), {})()
        rw.layer, rw.j, rw.pb, rw.pk, rw.TE = layer, j, pb, pk, TE
        NCH = TE // 64
        rw.NCH = NCH
        A = lambda name, shape, dt=F32: self.sb(st, name, shape, dt)
        rw.cols = A("rwcols", [128, 64])
        rw.rows_tile = A("rowst", [32, 128])
        rw.wa_up = A("wa_up", [128, 512])
        rw.g_up = A("g_up", [128, 512])
        rw.BD = A("BD", [128, 128])
        rw.hs = A("hs", [128, 2])
        rw.RKsel = A("RKsel", [128, 4, 2])
        rw.rmask = A("rmask", [128, TE])
        rw.MK = A("MK", [64, 4, 128])
        rw.ML = A("ML", [64, 8, 64])
        rw.I8 = A("I8", [64, 8, 64])
        rw.lnw_b = A("lnw_b", [64, 512])
        rw.lnb_b = A("lnb_b", [64, 512])
        rw.zb = A("zb", [128, 14, TE + 1])
        rw.zs = A("zs", [128, 14, TE])
        rw.ARt = A("ARt", [128, 4, 2, NCH, 2, 64])
        rw.BKt = A("BKt", [128, 4, 2, NCH, 2, 64])
        rw.nhs = A("nhs", [128, 2])
        rw.Ep = A("Ep", [128, 4, TE])
        rw.rk = A("rk", [128, 4, TE])
        rw.tnh = A("tnh", [128, TE])
        rw.sig = A("sig", [128, TE])
        for nm in ("t1", "ew", "cum", "Em", "Epv", "ta", "kk", "kp", "tt"):
            setattr(rw, nm, A(nm, [128, TE]))
        rw.GMb = A("GMb", [64, 8, 128])
        rw.GMk = A("GMk", [64, 8, 128])
        rw.Lm = A("Lm", [64, 8, 64])
        rw.LpA = A("LpA", [64, 8, 64]); rw.LpTA = A("LpTA", [64, 8, 64])
        rw.LpB = A("LpB", [64, 8, 64]); rw.LpTB = A("LpTB", [64, 8, 64])
        rw.NT = A("NT", [64, 8, 64])
        rw.Vc = A("Vc", [64, 512]); rw.Bc = A("Bc", [64, 512]); rw.Kc = A("Kc", [64, 512])
        rw.X = A("X", [64, 512]); rw.U = A("U", [64, 512])
        rw.yc = rw.U; rw.yt = rw.X
        rw.obc = [A("obc", [64, 512], BF16) for _ in range(2)]
        rw.mixTr = A("mixTr", [128, 4, TE], BF16)
        rw.st8 = A("st8", [64, 32])
        rw.ST = A("ST", [128, 4, 64]); rw.tmpS = A("tmpS", [128, 4, 64])
        vec = self.rwkv_vec[j]
        self.load_cols(st, rw.cols[:, 0:14], 'rwcols', self.rwkv_mu[j].rearrange("(c p) -> c p", p=128), 14,
                       pb[4], pk[4], rw.rows_tile)
        self.load_cols(st, rw.cols[:, 14:42], 'rwcols', vec.rearrange("v (c p) -> (v c) p", p=128), 28,
                       pb[5], pk[5], rw.rows_tile)
        rw.vcol = lambda v, hp: rw.cols[:, 14 + 4 * v + hp:15 + 4 * v + hp]
        sc.op('dve', lambda e: e.tensor_single_scalar(out=rw.cols[:, 42:46], in_=rw.cols[:, 14:18], scalar=-1.0, op=ALU.mult),
              r=['rwcols'], w=['rwcols'])
        sc.op('dve', lambda e: e.tensor_scalar(out=rw.cols[:, 46:50], in0=rw.cols[:, 26:30], scalar1=-1.0, scalar2=1.0,
                                             op0=ALU.mult, op1=ALU.add), r=['rwcols'], w=['rwcols'])
        sc.dma('sp', rw.wa_up[0:64, :], self.rwkv_w_up[j], w=['wa_up'])
        sc.dma('sp', rw.wa_up[64:128, :], self.rwkv_a_up[j], w=['wa_up'])
        sc.dma('sp', rw.g_up[:], self.rwkv_g_up[j], w=['g_up'])
        sc.dma('sp', rw.BD[:], self.c_bd[:], w=['BD'])
        sc.dma('sp', rw.hs[:], self.c_hs[:], w=['hs'])
        sc.dma('sp', rw.rmask[:], self.c_rmask[:, :TE], w=['rmask'])
        sc.dma('sp', rw.MK[:], self.c_mk[:], w=['MK'])
        sc.dma('sp', rw.ML[:], self.c_ml[:], w=['ML'])
        sc.dma('sp', rw.I8[:], self.c_i8[:], w=['I8'])
        sc.dma('sp', rw.lnw_b[:], vec[5:6, :].partition_broadcast(64), w=['lnw_b'])
        sc.dma('sp', rw.lnb_b[:], vec[6:7, :].partition_broadcast(64), w=['lnb_b'])
        sc.op('dve', lambda e: e.tensor_single_scalar(out=rw.nhs[:], in_=rw.hs[:], scalar=-1.0, op=ALU.mult), r=['hs'], w=['hs'])
        for hp in range(4):
            sc.op('dve', lambda e, hp=hp: e.tensor_scalar(out=rw.RKsel[:, hp, :], in0=rw.hs[:], scalar1=rw.vcol(4, hp),
                                                        scalar2=None, op0=ALU.mult), r=['hs', 'rwcols'], w=['RKsel'])
        sc.op('dve', lambda e: e.memset(rw.ST[:], 0.0), w=['ST'])
        sc.op('dve', lambda e: e.memset(rw.zb[:], 0.0), w=[('zb', c) for c in range(14)])
        if j == 1:
            rw.vdn = A("vdn", [128, 4, 32]); rw.vup = A("vup", [32, 512])
            rw.vfT = A("vfT", [128, 4, TE]); rw.vdT = A("vdT", [32, TE])
            self.load_cols(st, rw.cols[:, 50:54], 'rwcols', self.rwkv_v0[0].rearrange("(c p) -> c p", p=128), 4,
                           pb[6], pk[6], rw.rows_tile)
            sc.dma('sp', rw.vdn[:], self.rwkv_v_down[0].rearrange("(c p) r -> p c r", p=128), w=['vdn'])
            sc.dma('sp', rw.vup[:], self.rwkv_v_up[0], w=['vup'])
        return rw

    def rwkv_tile(self, rw, ti, w_in, wbufs, cnt, hT, nextpb, mixb):
        sc = self.sc
        TE, NCH, pb, pk, j = rw.TE, rw.NCH, rw.pb, rw.pk, rw.j
        zb, zs = rw.zb, rw.zs
        EXP, MUL, ADD, SUB = AF.Exp, ALU.mult, ALU.add, ALU.subtract

        def shift(cc):
            sc.op('dve', lambda e: e.tensor_tensor(out=rw.tt[:], in0=zb[:, cc, 0:TE], in1=zb[:, cc, 1:TE + 1], op=SUB),
                  r=[('zb', cc)], w=['tt'])
            sc.op('dve', lambda e: e.scalar_tensor_tensor(out=zs[:, cc, :], in0=rw.tt[:], scalar=rw.cols[:, cc:cc + 1],
                                                        in1=zb[:, cc, 1:TE + 1], op0=MUL, op1=ADD),
                  r=['tt', ('zb', cc), 'rwcols'], w=[('zs', cc)])
            sc.op('dve', lambda e: e.tensor_copy(out=zb[:, cc, 0:1], in_=zb[:, cc, TE:TE + 1]), r=[('zb', cc)], w=[('zb', cc)])

        for grp in range(7):
            def outs(ci, grp=grp):
                pa, pkk = nextpb()
                cc = grp * 2 + ci
                def evac(ci_, pa=pa, pkk=pkk, cc=cc):
                    sc.op('act', lambda e: e.copy(out=zb[:, cc, 1:TE + 1], in_=pa[:, :TE]), r=[pkk], w=[('zb', cc)])
                    shift(cc)
                return pa[:, :TE], pkk, evac
            self.proj_group(w_in, 1536 + grp * 256, 256, wbufs, cnt, hT, 'fm', outs)
        rcut = int(os.environ.get('RW_CUT', '99'))
        if rcut < 1:
            return
        sc.op('act', lambda e: e.activation(out=rw.tnh[0:64, :], in_=zs[0:64, 12, :], func=AF.Tanh), r=[('zs', 12)], w=['tnh'])
        sc.op('act', lambda e: e.activation(out=rw.sig[:], in_=zs[:, 13, :], func=AF.Sigmoid), r=[('zs', 13)], w=['sig'])
        pA, kA, pB, kB, pC, kC = pb[4], pk[4], pb[5], pk[5], pb[6], pk[6]
        v3 = lambda ap: ap.rearrange("p (c t) -> p c t", t=64)

        def prep(hp):
            hs_ = slice(hp * 128, (hp + 1) * 128)
            sc.op('pe', lambda e: e.matmul(pA[:, :TE], lhsT=rw.wa_up[0:64, hs_], rhs=rw.tnh[0:64, :], start=True, stop=True),
                  r=['wa_up', 'tnh'], w=[kA])
            sc.op('act', lambda e: e.activation(out=rw.t1[:], in_=pA[:, :TE], func=EXP, bias=rw.cols[:, 42 + hp:43 + hp], scale=-1.0),
                  r=[kA, 'rwcols'], w=['t1'])
            sc.op('act', lambda e: e.activation(out=rw.t1[:], in_=rw.t1[:], func=AF.Ln, bias=1.0), r=['t1'], w=['t1'])
            sc.op('act', lambda e: e.activation(out=rw.ew[:], in_=rw.t1[:], func=EXP, bias=-0.5, scale=-1.0), r=['t1'], w=['ew'])
            sc.op('pe', lambda e: e.matmul(pB[:, :TE], lhsT=rw.wa_up[64:128, hs_], rhs=zs[64:128, 12, :], start=True, stop=True),
                  r=['wa_up', ('zs', 12)], w=[kB])
            sc.op('act', lambda e: e.activation(out=rw.ta[:], in_=pB[:, :TE], func=AF.Sigmoid, bias=rw.vcol(1, hp)),
                  r=[kB, 'rwcols'], w=['ta'])
            sc.op('dve', lambda e: e.tensor_tensor_scan(out=rw.cum[:], data0=rw.rmask[:], data1=rw.ew[:], initial=0.0,
                                                      op0=MUL, op1=ADD), r=['rmask', 'ew'], w=['cum'])
            sc.op('act', lambda e: e.activation(out=rw.Ep[:, hp, :], in_=rw.cum[:], func=EXP, scale=-1.0), r=['cum'], w=[('Ep', hp)])
            sc.op('act', lambda e: e.activation(out=rw.Em[:], in_=rw.cum[:], func=EXP, scale=1.0), r=['cum'], w=['Em'])
            sc.op('dve', lambda e: e.tensor_tensor(out=rw.tt[:], in0=rw.cum[:], in1=rw.ew[:], op=SUB), r=['cum', 'ew'], w=['tt'])
            sc.op('act', lambda e: e.activation(out=rw.Epv[:], in_=rw.tt[:], func=EXP, scale=-1.0), r=['tt'], w=['Epv'])
            sc.op('dve', lambda e: e.tensor_scalar(out=rw.kk[:], in0=zs[:, 4 + hp, :], scalar1=rw.vcol(2, hp), scalar2=None, op0=MUL),
                  r=[('zs', 4 + hp), 'rwcols'], w=['kk'])
            sc.op('dve', lambda e: e.tensor_tensor(out=rw.tt[:], in0=rw.kk[:], in1=rw.kk[:], op=MUL), r=['kk'], w=['tt'])
            sc.op('pe', lambda e: e.matmul(pA[:, :TE], lhsT=rw.BD[:], rhs=rw.tt[:], start=True, stop=True), r=['BD', 'tt'], w=[kA])
            sc.op('act', lambda e: e.activation(out=rw.tt[:], in_=pA[:, :TE], func=AF.Sqrt), r=[kA], w=['tt'])
            sc.op('dve', lambda e: e.tensor_scalar_max(out=rw.tt[:], in0=rw.tt[:], scalar1=1e-12), r=['tt'], w=['tt'])
            sc.op('dve', lambda e: e.reciprocal(out=rw.tt[:], in_=rw.tt[:]), r=['tt'], w=['tt'])
            sc.op('dve', lambda e: e.tensor_tensor(out=rw.kk[:], in0=rw.kk[:], in1=rw.tt[:], op=MUL), r=['kk', 'tt'], w=['kk'])
            sc.op('dve', lambda e: e.tensor_scalar(out=rw.tt[:], in0=rw.ta[:], scalar1=rw.vcol(3, hp),
                                                 scalar2=rw.cols[:, 46 + hp:47 + hp], op0=MUL, op1=ADD),
                  r=['ta', 'rwcols'], w=['tt'])
            sc.op('dve', lambda e: e.tensor_tensor(out=rw.kp[:], in0=zs[:, 4 + hp, :], in1=rw.tt[:], op=MUL),
                  r=[('zs', 4 + hp), 'tt'], w=['kp'])
            sc.op('dve', lambda e: e.tensor_tensor(out=rw.tt[:], in0=rw.kk[:], in1=rw.ta[:], op=MUL), r=['kk', 'ta'], w=['tt'])
            for par in range(2):
                def masked(par):
                    m_ = rw.hs[:, par:par + 1]
                    nm_ = rw.nhs[:, par:par + 1]
                    sc.op('dve', lambda e: e.scalar_tensor_tensor(out=rw.ARt[:, hp, par, :, 0, :], in0=v3(rw.kk[:]), scalar=nm_,
                                                                in1=v3(rw.Epv[:]), op0=MUL, op1=MUL), r=['kk', 'Epv', 'hs'], w=[('ARt', hp)])
                    sc.op('dve', lambda e: e.scalar_tensor_tensor(out=rw.ARt[:, hp, par, :, 1, :], in0=v3(zs[:, hp, :]), scalar=m_,
                                                                in1=v3(rw.Ep[:, hp, :]), op0=MUL, op1=MUL),
                          r=[('zs', hp), ('Ep', hp), 'hs'], w=[('ARt', hp)])
                    sc.op('dve', lambda e: e.scalar_tensor_tensor(out=rw.BKt[:, hp, par, :, 0, :], in0=v3(rw.tt[:]), scalar=m_,
                                                                in1=v3(rw.Em[:]), op0=MUL, op1=MUL), r=['tt', 'Em', 'hs'], w=[('BKt', hp)])
                    sc.op('dve', lambda e: e.scalar_tensor_tensor(out=rw.BKt[:, hp, par, :, 1, :], in0=v3(rw.kp[:]), scalar=m_,
                                                                in1=v3(rw.Em[:]), op0=MUL, op1=MUL), r=['kp', 'Em', 'hs'], w=[('BKt', hp)])
                masked(par)
            sc.op('dve', lambda e: e.tensor_tensor(out=rw.rk[:, hp, :], in0=zs[:, hp, :], in1=rw.kp[:], op=MUL),
                  r=[('zs', hp), 'kp'], w=[('rk', hp)])

        vfv = self.vfirst.rearrange("(c p) s -> p c s", p=128)[:, :, ti * TE:(ti + 1) * TE]
        vkeys = [('zs', 8 + hp) for hp in range(4)]
        if j == 0:
            if os.environ.get('NOVF', '') == '':
                sc.dma('sp', vfv, zs[:, 8:12, :], r=vkeys, w=[('vf', ti)])
        else:
            sc.dma('sp', rw.vfT[:], vfv, r=[('vf', ti)], w=['vfT'])
            for hp in range(4):
                sc.op('pe', lambda e, hp=hp: e.matmul(pB[:32, 0:TE], lhsT=rw.vdn[:, hp, :], rhs=zs[:, 8 + hp, :],
                                                    start=(hp == 0), stop=(hp == 3)), r=['vdn'] + vkeys, w=[kB])
            sc.op('act', lambda e: e.copy(out=rw.vdT[:], in_=pB[:32, 0:TE]), r=[kB], w=['vdT'])
            for hp in range(4):
                def vres(hp):
                    sc.op('pe', lambda e: e.matmul(pC[:, 0:TE], lhsT=rw.vup[:, hp * 128:(hp + 1) * 128], rhs=rw.vdT[:],
                                                 start=True, stop=True), r=['vdT', 'vup'], w=[kC])
                    sc.op('act', lambda e: e.activation(out=rw.tt[:], in_=pC[:, 0:TE], func=AF.Sigmoid,
                                                       bias=rw.cols[:, 50 + hp:51 + hp]), r=[kC, 'rwcols'], w=['tt'])
                    sc.op('dve', lambda e: e.tensor_tensor(out=rw.vfT[:, hp, :], in0=rw.vfT[:, hp, :], in1=zs[:, 8 + hp, :], op=SUB),
                          r=['vfT', ('zs', 8 + hp)], w=['vfT'])
                    sc.op('dve', lambda e: e.tensor_tensor(out=rw.vfT[:, hp, :], in0=rw.vfT[:, hp, :], in1=rw.tt[:], op=MUL),
                          r=['vfT', 'tt'], w=['vfT'])
                    sc.op('dve', lambda e: e.tensor_tensor(out=zs[:, 8 + hp, :], in0=zs[:, 8 + hp, :], in1=rw.vfT[:, hp, :], op=ADD),
                          r=['vfT', ('zs', 8 + hp)], w=[('zs', 8 + hp)])
                vres(hp)
        for hp in range(4):
            prep(hp)
        if rcut < 2:
            return
        ARk = [('ARt', hp) for hp in range(4)]
        BKk = [('BKt', hp) for hp in range(4)]

        def mm8(bank, bkey, lhs_fn, rhs_fn, rkeys, start=True, stop=True):
            for h in range(8):
                sc.op('pe', lambda e, h=h: e.matmul(bank[:64, h * 64:(h + 1) * 64], lhsT=lhs_fn(h), rhs=rhs_fn(h),
                                                  start=start, stop=stop), r=rkeys, w=[bkey])

        def chunk(ch):
            csl = slice(ch * 64, (ch + 1) * 64)
            gch = ti * NCH + ch
            At = lambda h: rw.ARt[:, h // 2, h % 2, ch, 0, :]
            Rt = lambda h: rw.ARt[:, h // 2, h % 2, ch, 1, :]
            AR = lambda h: rw.ARt[:, h // 2, h % 2, ch, :, :].rearrange("p a t -> p (a t)")
            Bt = lambda h: rw.BKt[:, h // 2, h % 2, ch, 0, :]
            Kt = lambda h: rw.BKt[:, h // 2, h % 2, ch, 1, :]
            STh = lambda h: rw.ST[:, h // 2, :]
            hsl = lambda h: slice(h * 64, (h + 1) * 64)
            for (dst, dkey, bank, bkey, srcs, skeys, eng) in (
                    (rw.Vc, 'Vc', pA, kA, lambda hp: [zs[:, 8 + hp, csl]], [('zs', 8 + hp) for hp in range(4)], 'act'),
                    (rw.Bc, 'Bc', pB, kB, lambda hp: [rw.BKt[:, hp, 0, ch, 0, :], rw.BKt[:, hp, 1, ch, 0, :]], BKk, 'dve'),
                    (rw.Kc, 'Kc', pC, kC, lambda hp: [rw.BKt[:, hp, 0, ch, 1, :], rw.BKt[:, hp, 1, ch, 1, :]], BKk, 'act')):
                for hp in range(4):
                    lst = srcs(hp)
                    for qi, src_ap in enumerate(lst):
                        sc.op('pe', lambda e, hp=hp, bank=bank, src_ap=src_ap, qi=qi, nl=len(lst): e.matmul(
                            bank[:64, hp * 128:(hp + 1) * 128], lhsT=src_ap, rhs=self.ident_f[:],
                            start=(qi == 0), stop=(qi == nl - 1)), r=skeys + ['identf'], w=[bkey])
                if eng == 'act':
                    sc.op('act', lambda e, dst=dst, bank=bank: e.copy(out=dst[:], in_=bank[:64, :]), r=[bkey], w=[dkey])
                else:
                    sc.op('dve', lambda e, dst=dst, bank=bank: e.tensor_copy(out=dst[:], in_=bank[:64, :]), r=[bkey], w=[dkey])
            if rcut < 4:
                return
            for q in range(2):
                for (bank, bkey, lf, GM, gkey) in ((pA, kA, Bt, rw.GMb, 'GMb'), (pB, kB, Kt, rw.GMk, 'GMk')):
                    for hh in range(4):
                        h = 4 * q + hh
                        sc.op('pe', lambda e, h=h, hh=hh, bank=bank, lf=lf: e.matmul(
                            bank[:64, hh * 128:(hh + 1) * 128], lhsT=lf(h), rhs=AR(h), start=True, stop=True),
                              r=ARk + BKk, w=[bkey])
                    sc.op('dve', lambda e, q=q, bank=bank, GM=GM: e.tensor_tensor(
                        out=GM[:, 4 * q:4 * q + 4, :], in0=bank[:64, :].rearrange("p (h t) -> p h t", t=128),
                        in1=rw.MK[:], op=MUL), r=[bkey, 'MK'], w=[gkey])
            mm8(pC, kC, At, Bt, ARk + BKk)
            sc.op('dve', lambda e: e.tensor_tensor(out=rw.Lm[:], in0=pC[:64, :].rearrange("p (h t) -> p h t", t=64),
                                                 in1=rw.ML[:], op=MUL), r=[kC, 'ML'], w=['Lm'])
            if rcut < 5:
                return
            sc.op('dve', lambda e: e.tensor_tensor(out=rw.NT[:], in0=rw.I8[:], in1=rw.GMb[:, :, 0:64], op=ADD),
                  r=['I8', 'GMb'], w=['NT'])
            Lp, LpT, kLp, kLpT = rw.Lm, rw.GMb[:, :, 0:64], 'Lm', 'GMb'
            bufs = [(rw.LpA, rw.LpTA, 'LpA', 'LpTA'), (rw.LpB, rw.LpTB, 'LpB', 'LpTB')]
            for it in range(5):
                nLp, nLpT, knLp, knLpT = bufs[it % 2]
                mm8(pA, kA, (lambda h, LpT=LpT: LpT[:, h, :]), (lambda h, Lp=Lp: Lp[:, h, :]), [kLp, kLpT])
                mm8(pB, kB, (lambda h, Lp=Lp: Lp[:, h, :]), (lambda h, LpT=LpT: LpT[:, h, :]), [kLp, kLpT])
                sc.op('act', lambda e, nLp=nLp: e.copy(out=nLp[:].rearrange("p h t -> p (h t)"), in_=pA[:64, :]), r=[kA], w=[knLp])
                sc.op('dve', lambda e, nLpT=nLpT: e.tensor_copy(out=nLpT[:].rearrange("p h t -> p (h t)"), in_=pB[:64, :]),
                      r=[kB], w=[knLpT])
                mm8(pC, kC, (lambda h, nLp=nLp: nLp[:, h, :]), (lambda h: rw.NT[:, h, :]), [knLp, 'NT'])
                sc.op('dve', lambda e: e.tensor_tensor(out=rw.NT[:].rearrange("p h t -> p (h t)"),
                                                     in0=rw.NT[:].rearrange("p h t -> p (h t)"), in1=pC[:64, :], op=ADD),
                      r=[kC, 'NT'], w=['NT'])
                Lp, LpT, kLp, kLpT = nLp, nLpT, knLp, knLpT
            if rcut < 6:
                return
            def mmseq(bank, bkey, terms, rkeys):
                for h in range(8):
                    for i_, (lf_, rf_) in enumerate(terms):
                        sc.op('pe', lambda e, h=h, lf_=lf_, rf_=rf_, i_=i_: e.matmul(
                            bank[:64, h * 64:(h + 1) * 64], lhsT=lf_(h), rhs=rf_(h),
                            start=(i_ == 0), stop=(i_ == len(terms) - 1)), r=rkeys, w=[bkey])
            mmseq(pA, kA, [(At, STh), ((lambda h: rw.GMk[:, h, 0:64]), (lambda h: rw.Vc[:, hsl(h)]))],
                  ARk + ['ST', 'GMk', 'Vc'])
            sc.op('act', lambda e: e.copy(out=rw.X[:], in_=pA[:64, :]), r=[kA], w=['X'])
            mm8(pB, kB, (lambda h: rw.NT[:, h, :]), (lambda h: rw.X[:, hsl(h)]), ['NT', 'X'])
            sc.op('dve', lambda e: e.tensor_copy(out=rw.U[:], in_=pB[:64, :]), r=[kB], w=['U'])
            mmseq(pC, kC, [(Rt, STh), ((lambda h: rw.GMb[:, h, 64:128]), (lambda h: rw.U[:, hsl(h)])),
                           ((lambda h: rw.GMk[:, h, 64:128]), (lambda h: rw.Vc[:, hsl(h)]))],
                  ARk + ['ST', 'GMb', 'U', 'GMk', 'Vc'])
            if rcut < 7:
                return
            for hp in range(4):
                hs_ = slice(hp * 128, (hp + 1) * 128)
                sc.op('pe', lambda e, hs_=hs_: e.matmul(pA[:, hs_], lhsT=rw.Bc[:, hs_], rhs=rw.U[:, hs_], start=True, stop=False),
                      r=['Bc', 'U'], w=[kA])
                sc.op('pe', lambda e, hs_=hs_: e.matmul(pA[:, hs_], lhsT=rw.Kc[:, hs_], rhs=rw.Vc[:, hs_], start=False, stop=True),
                      r=['Kc', 'Vc'], w=[kA])
            pA3 = pA[:].rearrange("p (h t) -> p h t", t=128)
            sc.op('dve', lambda e: e.tensor_tensor(out=rw.tmpS[0:64, :, :], in0=rw.ST[0:64, :, :], in1=pA3[0:64, :, 0:64], op=ADD),
                  r=[kA, 'ST'], w=['tmpS'])
            sc.op('dve', lambda e: e.tensor_tensor(out=rw.tmpS[64:128, :, :], in0=rw.ST[64:128, :, :], in1=pA3[64:128, :, 64:128], op=ADD),
                  r=[kA, 'ST'], w=['tmpS'])
            for hp in range(4):
                sc.op('dve', lambda e, hp=hp: e.tensor_scalar(out=rw.ST[:, hp, :], in0=rw.tmpS[:, hp, :],
                                                            scalar1=rw.Ep[:, hp, ch * 64 + 63:ch * 64 + 64], scalar2=None, op0=MUL),
                      r=['tmpS', ('Ep', hp)], w=['ST'])
            if rcut < 8:
                return
            Y3 = pC[:64, :].rearrange("p (h t) -> p h t", t=64)
            yc3 = rw.yc[:].rearrange("p (h t) -> p h t", t=64)
            yt3 = rw.yt[:].rearrange("p (h t) -> p h t", t=64)
            s8 = rw.st8
            bc = lambda ap: ap.unsqueeze(2).to_broadcast([64, 8, 64])
            sc.op('dve', lambda e: e.reduce_sum(out=s8[:, 0:8], in_=Y3, axis=AX.X), r=[kC], w=['s8a'])
            sc.op('dve', lambda e: e.tensor_single_scalar(out=s8[:, 0:8], in_=s8[:, 0:8], scalar=1.0 / 64, op=MUL), r=['s8a'], w=['s8a'])
            sc.op('dve', lambda e: e.tensor_tensor(out=yc3, in0=Y3, in1=bc(s8[:, 0:8]), op=SUB), r=[kC, 's8a'], w=['U'])
            sc.op('dve', lambda e: e.tensor_tensor(out=rw.yt[:], in0=rw.yc[:], in1=rw.yc[:], op=MUL), r=['U'], w=['X'])
            sc.op('dve', lambda e: e.reduce_sum(out=s8[:, 8:16], in_=yt3, axis=AX.X), r=['X'], w=['s8b'])
            self.rsqrt(s8[:, 8:16], 's8b', 1.0 / 64, 64e-5)
            sc.op('dve', lambda e: e.tensor_tensor(out=yc3, in0=yc3, in1=bc(s8[:, 8:16]), op=MUL), r=['U', 's8b'], w=['U'])
            sc.op('dve', lambda e: e.tensor_tensor(out=rw.yc[:], in0=rw.yc[:], in1=rw.lnw_b[:], op=MUL), r=['U', 'lnw_b'], w=['U'])
            sc.op('dve', lambda e: e.tensor_tensor(out=rw.yc[:], in0=rw.yc[:], in1=rw.lnb_b[:], op=ADD), r=['U', 'lnb_b'], w=['U'])
            for hp in range(4):
                sc.op('pe', lambda e, hp=hp: e.matmul(pB[:64, 2 * hp:2 * hp + 2], lhsT=rw.rk[:, hp, csl], rhs=rw.RKsel[:, hp, :],
                                                    start=True, stop=True), r=[('rk', hp), 'RKsel'], w=[kB])
            sc.op('act', lambda e: e.copy(out=s8[:, 16:24], in_=pB[:64, 0:8]), r=[kB], w=['s8c'])
            sc.op('dve', lambda e: e.tensor_tensor(out=yt3, in0=rw.Vc[:].rearrange("p (h t) -> p h t", t=64), in1=bc(s8[:, 16:24]), op=MUL),
                  r=['Vc', 's8c'], w=['X'])
            sc.op('dve', lambda e: e.tensor_tensor(out=rw.yc[:], in0=rw.yc[:], in1=rw.yt[:], op=ADD), r=['U', 'X'], w=['U'])
            sc.op('pe', lambda e: e.matmul(pA[:64, :], lhsT=rw.sig[:, csl], rhs=rw.g_up[:], start=True, stop=True),
                  r=['sig', 'g_up'], w=[kA])
            ob = rw.obc[gch % 2]
            okey = ('obc', gch % 2)
            sc.op('dve', lambda e: e.tensor_tensor(out=ob[:], in0=rw.yc[:], in1=pA[:64, :], op=MUL), r=['U', kA], w=[okey])
            for c in range(4):
                sc.op('pe', lambda e, c=c: e.transpose(out=rw.psT[:, c * 64:(c + 1) * 64], in_=ob[:, c * 128:(c + 1) * 128],
                                                     identity=self.ident_bf[0:64, 0:64]), r=[okey, 'identb'], w=[('psT', 0)])
            sc.op('act', lambda e: e.copy(out=rw.mixTr[:, :, ch * 64:(ch + 1) * 64],
                                         in_=rw.psT[:, 0:256].rearrange("p (c t) -> p c t", t=64)),
                  r=[('psT', 0)], w=['mixTr'])

        for ch in range(NCH):
            chunk(ch)

    def build(self, phases):
        nc = self.nc
        S = self.S
        self.x_in = self.din("x", [S, D])
        self.norms = self.din("norms", [4, 6, D])
        self.w_gu = self.din("ffn_wgu", [4, 2, D, 2 * DFF])
        self.w_d = self.din("ffn_wd", [4, 2, DFF, D])
        self.c_ident = self.din("c_ident", [128, 128])
        self.odd_w_in = self.din("odd_w_in", [2, D, 6144])
        self.odd_w_out = self.din("odd_w_out", [2, 2048, D])
        self.c_ret_maskT = self.din("c_ret_maskT", [4, 128, 128])
        self.c_ret_qdrow = self.din("c_ret_qdrow", [4, 128, T])
        self.c_ret_kd = self.din("c_ret_kd", [128, 4])
        self.even_w_in = self.din("even_w_in", [2, D, 3328])
        self.even_w_out = self.din("even_w_out", [2, D, D])
        self.diff_lam = self.din("diff_lam", [2, 4, 64])
        self.diff_subln = self.din("diff_subln", [2, 128])
        self.c_att_bias = self.din("c_att_bias", [128, 4, 32])
        self.rwkv_mu = self.din("rwkv_mu", [2, 1792])
        self.rwkv_vec = self.din("rwkv_vec", [2, 7, 512])
        self.rwkv_w_up = self.din("rwkv_w_up", [2, 64, 512])
        self.rwkv_a_up = self.din("rwkv_a_up", [2, 64, 512])
        self.rwkv_g_up = self.din("rwkv_g_up", [2, 128, 512])
        self.rwkv_v0 = self.din("rwkv_v0", [1, 512])
        self.rwkv_v_down = self.din("rwkv_v_down", [1, 512, 32])
        self.rwkv_v_up = self.din("rwkv_v_up", [1, 32, 512])
        self.c_bd = self.din("c_bd", [128, 128])
        self.c_hs = self.din("c_hs", [128, 2])
        self.c_rmask = self.din("c_rmask", [128, 256])
        self.c_mk = self.din("c_mk", [64, 4, 128])
        self.c_ml = self.din("c_ml", [64, 8, 64])
        self.c_i8 = self.din("c_i8", [64, 8, 64])
        self.vfirst = nc.dram_tensor("vfirst", [512, S], F32, kind="ExternalOutput").ap()
        self.c_att_diag = self.din("c_att_diag", [128, 4, 128])
        self.dbg_on = os.environ.get('DBG', '') != ''
        if self.dbg_on:
            self.dbg = nc.dram_tensor("dbg", [S, D], F32, kind="ExternalOutput").ap()
        else:
            self.dbg = self.x_in
        self.out = nc.dram_tensor("out", [S, D], F32, kind="ExternalOutput").ap()
        with ExitStack() as st:
            self.load_consts(st)
            if os.environ.get('NO_PRECAST', '') == '':
                def precast(name, ap3):
                    L_, R_, C_ = ap3.shape
                    dst = nc.dram_tensor(name + "_bf16", [L_, R_, C_], BF16, kind="ExternalOutput").ap()
                    for l_ in range(L_):
                        for r0 in range(0, R_, 128):
                            self.sc.dma('pool', dst[l_, r0:r0 + 128, :], ap3[l_, r0:r0 + 128, :], w=[(name, l_, r0)])
                    return dst
                self.w_gu = precast("wgu", self.w_gu.rearrange("a b r c -> (a b) r c")).rearrange("(a b) r c -> a b r c", b=2)
                self.w_d = precast("wd", self.w_d.rearrange("a b r c -> (a b) r c")).rearrange("(a b) r c -> a b r c", b=2)
                self.even_w_in = precast("ewi", self.even_w_in)
                self.even_w_out = precast("ewo", self.even_w_out)
                self.odd_w_in = precast("owi", self.odd_w_in)
                self.odd_w_out = precast("owo", self.odd_w_out)
                self.sc.barrier()
            src = self.x_in
            for ph in phases:
                if ph[0] == 'ffn':
                    self.ffn_phase(ph[1], ph[2], src, self.out)
                elif ph[0] == 'ret':
                    self.ret_phase(ph[1], src, self.out)
                elif ph[0] == 'even':
                    self.even_phase(ph[1], src, self.out)
                src = self.out
            self.sc.emit(st)
        return nc


def host_consts():
    c = {"c_ident": np.eye(128, dtype=np.float32)}
    idx = np.arange(128)
    i = idx[None, :]
    jj = idx[:, None]
    maskT = np.zeros((4, 128, 128), np.float64)
    qd = np.zeros((4, 128), np.float64)
    kd = np.zeros((128, 4), np.float64)
    for h in range(4):
        g = 1.0 - 2.0 ** (-5.0 - h)
        same = (i // 64) == (jj // 64)
        cross = (jj < 64) & (i >= 64)
        maskT[h] = np.where(same, g ** np.abs(i - jj), np.where(cross, g ** (i - jj).clip(0), 0.0))
        qd[h] = g ** (idx + 1.0)
        kd[:, h] = g ** (127.0 - idx) / 16.0
    c["c_ret_maskT"] = maskT.astype(np.float32)
    c["c_ret_qdrow"] = np.ascontiguousarray(
        np.broadcast_to(np.tile(qd, (1, T // 128))[:, None, :], (4, 128, T))).astype(np.float32)
    c["c_ret_kd"] = kd.astype(np.float32)
    ab = np.zeros((128, 4, 32), np.float64)
    ad = np.zeros((128, 4, 128), np.float64)
    kl = idx[:, None]
    ql = idx[None, :]
    for h in range(4):
        sl = 2.0 ** (-8.0 / 4 * (h + 1))
        for dl in range(32):
            ab[:, h, dl] = -sl * (128.0 * dl - idx)
        allowed = (kl // 64) <= (ql // 64)
        ad[:, h, :] = np.where(allowed, -sl * np.abs(ql - kl) + sl * ql, -30000.0)
    c["c_att_bias"] = ab.astype(np.float32)
    p = np.arange(128)
    c["c_bd"] = ((p[:, None] // 64) == (p[None, :] // 64)).astype(np.float32)
    c["c_hs"] = np.stack([(p < 64), (p >= 64)], axis=1).astype(np.float32)
    c["c_rmask"] = np.broadcast_to(((np.arange(256) % 64) != 0).astype(np.float32)[None, :], (128, 256)).copy()
    s64 = np.arange(64)[:, None]
    t64 = np.arange(64)[None, :]
    mk = np.concatenate([(s64 < t64), (s64 <= t64)], axis=1).astype(np.float32)
    c["c_mk"] = np.broadcast_to(mk[:, None, :], (64, 4, 128)).copy()
    c["c_ml"] = np.broadcast_to((s64 > t64).astype(np.float32)[:, None, :], (64, 8, 64)).copy()
    c["c_i8"] = np.broadcast_to(np.eye(64, dtype=np.float32)[:, None, :], (64, 8, 64)).copy()
    c["c_att_diag"] = ad.astype(np.float32)
    return c


_CACHE = {}


def kernel(**inputs):
    x = np.ascontiguousarray(inputs['x'], dtype=np.float32)
    B, S, _ = x.shape
    phases = []
    for i in range(4):
        phases += [('ffn', i, 0), ('even', i) if i % 2 == 0 else ('ret', i), ('ffn', i, 1)]
    b = Builder(S)
    nc = b.build(phases)
    names = set(b.in_names)
    shared = {k: np.ascontiguousarray(v, dtype=np.float32) for k, v in inputs.items() if k in names and k != 'x'}
    shared.update({k: v for k, v in host_consts().items() if k in names})
    in_maps = [dict(shared, x=x[i]) for i in range(B)]
    res = run_bass_kernel_spmd(nc, in_maps, core_ids=list(range(B)))
    return np.stack([np.asarray(r["out"]) for r in res.results], axis=0)
```
